# Optimizing a Trainium2 kernel written in Bass

```python
import jax, jax.numpy as jnp
from jax import lax
import numpy as np

D_MODEL = 1024
BATCH = 4
SEQ = 4096
DEPTH = 1

GRID_W = 64
EPS = 1e-6
GLA_HEADS = 4
GLA_DK = D_MODEL // (2 * GLA_HEADS)
GLA_DV = D_MODEL // GLA_HEADS
GLA_QK = GLA_HEADS * GLA_DK
GLA_V = GLA_HEADS * GLA_DV
GLA_RANK = 16
GLA_GATE_NORM = 16.0
GLA_CHUNK = 64
ATT_HD = 128
ATT_HEADS = D_MODEL // ATT_HD
ATT_KV_HEADS = 2
ATT_Q = ATT_HEADS * ATT_HD
ATT_KV = ATT_KV_HEADS * ATT_HD
ATT_BLOCK = 128
ROPE_THETA = 10000.0
ROPE_AXIS_DIM = ATT_HD // 2
D_FF = -(-8 * D_MODEL // (3 * 256)) * 256
IN_SPLITS = (GLA_QK, GLA_QK, GLA_V, GLA_V, GLA_RANK, GLA_RANK,
             ATT_Q, ATT_KV, ATT_KV, D_MODEL, D_MODEL)
D_IN = sum(IN_SPLITS)

kernel_name = "hybrid_gla_axial_gqa_gated_encoder"


def rmsnorm(x, g):
    xf = x.astype(jnp.float32)
    y = xf * lax.rsqrt(jnp.mean(xf * xf, axis=-1, keepdims=True) + EPS)
    return (y * g.astype(jnp.float32)).astype(x.dtype)


def split_points():
    return [int(v) for v in np.cumsum(IN_SPLITS)[:-1]]


def gla_scan(q, k, v, log_a, strict):
    B, H, S, dk = q.shape
    dv = v.shape[-1]
    n = S // GLA_CHUNK

    def to_chunks(t):
        return jnp.moveaxis(t.astype(jnp.float32).reshape(B, H, n, GLA_CHUNK, t.shape[-1]), 2, 0)

    qc, kc, vc, ac = to_chunks(q), to_chunks(k), to_chunks(v), to_chunks(log_a)
    idx = jnp.arange(GLA_CHUNK)
    mask = (idx[:, None] > idx[None, :]) if strict else (idx[:, None] >= idx[None, :])

    def step(state, inp):
        qi, ki, vi, ai = inp
        cum = jnp.cumsum(ai, axis=2)
        diff = cum[:, :, :, None, :] - cum[:, :, None, :, :]
        decay = jnp.exp(jnp.where(mask[None, None, :, :, None], diff, -jnp.inf))
        scores = jnp.einsum('bhid,bhijd,bhjd->bhij', qi, decay, ki)
        o_intra = jnp.einsum('bhij,bhjv->bhiv', scores, vi)
        o_inter = jnp.einsum('bhid,bhdv->bhiv', qi * jnp.exp(cum), state)
        last = cum[:, :, -1:, :]
        k_dec = ki * jnp.exp(last - cum)
        state = state * jnp.exp(last[:, :, 0, :])[..., None] + jnp.einsum('bhjd,bhjv->bhdv', k_dec, vi)
        return state, o_intra + o_inter

    state0 = jnp.zeros((B, H, dk, dv), jnp.float32)
    _, out = lax.scan(step, state0, (qc, kc, vc, ac))
    return jnp.moveaxis(out, 0, 2).reshape(B, H, S, dv)


def axial_rope_tables(seq_len):
    rows = seq_len // GRID_W
    row = jnp.repeat(jnp.arange(rows), GRID_W).astype(jnp.float32)
    col = jnp.tile(jnp.arange(GRID_W), rows).astype(jnp.float32)
    freqs = ROPE_THETA ** (-jnp.arange(0, ROPE_AXIS_DIM, 2, dtype=jnp.float32) / ROPE_AXIS_DIM)
    ang_r = row[:, None] * freqs[None, :]
    ang_c = col[:, None] * freqs[None, :]
    return jnp.cos(ang_r), jnp.sin(ang_r), jnp.cos(ang_c), jnp.sin(ang_c)


def rope_half(x, cos, sin):
    c, s = cos[:, None, :], sin[:, None, :]
    x1, x2 = jnp.split(x, 2, axis=-1)
    return jnp.concatenate([x1 * c - x2 * s, x2 * c + x1 * s], axis=-1)


def apply_axial_rope(x, tables):
    cr, sr, cc, sc = tables
    xf = x.astype(jnp.float32)
    xr, xc = xf[..., :ROPE_AXIS_DIM], xf[..., ROPE_AXIS_DIM:]
    return jnp.concatenate([rope_half(xr, cr, sr), rope_half(xc, cc, sc)], axis=-1)


def block_attention(q, k, v):
    B, Hkv, G, S, hd = q.shape
    nb = S // ATT_BLOCK
    qb = jnp.moveaxis(q.reshape(B, Hkv, G, nb, ATT_BLOCK, hd), 3, 0)
    scale = hd ** -0.5

    def one(qi):
        s = jnp.einsum('bkgqd,bksd->bkgqs', qi, k) * scale
        p = jax.nn.softmax(s, axis=-1)
        return jnp.einsum('bkgqs,bksd->bkgqd', p, v)

    o = lax.map(one, qb)
    return jnp.moveaxis(o, 0, 3).reshape(B, Hkv, G, S, hd)


def setup_inputs(seed: int = 0) -> dict:
    key = jax.random.key(seed)
    ks = jax.random.split(key, 20)
    f32 = jnp.float32
    nrm = lambda k, shape, s: (jax.random.normal(k, shape, f32) * s)
    gain = lambda k, shape: 1.0 + 0.05 * jax.random.normal(k, shape, f32)
    return {
        "x": jax.random.normal(ks[0], (BATCH, SEQ, D_MODEL), f32),
        "mix_norm": gain(ks[1], (DEPTH, D_MODEL)),
        "w_in": nrm(ks[2], (DEPTH, D_MODEL, D_IN), D_MODEL ** -0.5),
        "w_gk_fwd": nrm(ks[3], (DEPTH, GLA_RANK, GLA_QK), GLA_RANK ** -0.5),
        "b_gk_fwd": nrm(ks[4], (DEPTH, GLA_QK), 0.1),
        "w_gk_bwd": nrm(ks[5], (DEPTH, GLA_RANK, GLA_QK), GLA_RANK ** -0.5),
        "b_gk_bwd": nrm(ks[6], (DEPTH, GLA_QK), 0.1),
        "gla_norm": gain(ks[7], (DEPTH, GLA_DV)),
        "q_norm": gain(ks[8], (DEPTH, ATT_HD)),
        "k_norm": gain(ks[9], (DEPTH, ATT_HD)),
        "b_gate": nrm(ks[10], (DEPTH, 2, D_MODEL), 0.1),
        "w_o": nrm(ks[11], (DEPTH, D_MODEL, D_MODEL), D_MODEL ** -0.5),
        "ffn_norm": gain(ks[12], (DEPTH, D_MODEL)),
        "w_gate_up": nrm(ks[13], (DEPTH, D_MODEL, 2 * D_FF), D_MODEL ** -0.5),
        "w_down": nrm(ks[14], (DEPTH, D_FF, D_MODEL), D_FF ** -0.5),
        "final_norm": gain(ks[15], (D_MODEL,)),
    }


def reference(x, mix_norm, w_in, w_gk_fwd, b_gk_fwd, w_gk_bwd, b_gk_bwd, gla_norm,
              q_norm, k_norm, b_gate, w_o, ffn_norm, w_gate_up, w_down, final_norm):
    B, S, _ = x.shape
    dt = x.dtype
    rope = axial_rope_tables(S)
    points = split_points()

    def heads(t, n):
        return t.reshape(B, S, n, -1).transpose(0, 2, 1, 3)

    for l in range(DEPTH):
        h = rmsnorm(x, mix_norm[l])
        proj = h @ w_in[l]
        (g_q, g_k, g_v, g_out, g_rf, g_rb,
         a_q, a_k, a_v, gate_a, gate_b) = jnp.split(proj, points, axis=-1)

        qg = heads(g_q, GLA_HEADS) * (GLA_DK ** -0.5)
        kg = heads(g_k, GLA_HEADS)
        vg = heads(g_v, GLA_HEADS)
        la_f = heads(jax.nn.log_sigmoid((g_rf @ w_gk_fwd[l] + b_gk_fwd[l]).astype(jnp.float32)) / GLA_GATE_NORM, GLA_HEADS)
        la_b = heads(jax.nn.log_sigmoid((g_rb @ w_gk_bwd[l] + b_gk_bwd[l]).astype(jnp.float32)) / GLA_GATE_NORM, GLA_HEADS)
        o_f = gla_scan(qg, kg, vg, la_f, strict=False)
        flip = lambda t: jnp.flip(t, axis=2)
        o_b = flip(gla_scan(flip(qg), flip(kg), flip(vg), flip(la_b), strict=True))
        o_a = rmsnorm(o_f + o_b, gla_norm[l])
        y_a = (o_a.transpose(0, 2, 1, 3).reshape(B, S, GLA_V) * jax.nn.silu(g_out.astype(jnp.float32))).astype(dt)

        qa = rmsnorm(a_q.reshape(B, S, ATT_HEADS, ATT_HD), q_norm[l])
        ka = rmsnorm(a_k.reshape(B, S, ATT_KV_HEADS, ATT_HD), k_norm[l])
        qa = apply_axial_rope(qa, rope)
        ka = apply_axial_rope(ka, rope)
        G = ATT_HEADS // ATT_KV_HEADS
        qa = qa.reshape(B, S, ATT_KV_HEADS, G, ATT_HD).transpose(0, 2, 3, 1, 4)
        ka = ka.transpose(0, 2, 1, 3)
        va = heads(a_v, ATT_KV_HEADS).astype(jnp.float32)
        o_att = block_attention(qa, ka, va)
        y_b = o_att.transpose(0, 3, 1, 2, 4).reshape(B, S, ATT_Q).astype(dt)

        sg_a = jax.nn.sigmoid(gate_a + b_gate[l, 0])
        sg_b = jax.nn.sigmoid(gate_b + b_gate[l, 1])
        mixed = sg_a * y_a + sg_b * y_b
        x = x + mixed @ w_o[l]

        h2 = rmsnorm(x, ffn_norm[l])
        gu = h2 @ w_gate_up[l]
        gt, up = jnp.split(gu, 2, axis=-1)
        x = x + (jax.nn.silu(gt) * up) @ w_down[l]

    return rmsnorm(x, final_norm)
```

```python
from contextlib import ExitStack
import numpy as np
import concourse.bass as bass
import concourse.mybir as mybir
from concourse.bass_utils import run_bass_kernel_spmd

F32 = mybir.dt.float32
BF16 = mybir.dt.bfloat16
AF = mybir.ActivationFunctionType
ALU = mybir.AluOpType
AX = mybir.AxisListType

D = 1024
SEQ = 4096
OWN = 2048
NT = 32
NTO = 16
DFF = 2816
NJ = 22
EPS = 1e-6
GRID_W = 64
C_GQ, C_GK, C_GV, C_GO, C_RF, C_RB, C_AQ, C_AK, C_AV, C_GA, C_GB = (
    0, 512, 1024, 2048, 3072, 3088, 3104, 4128, 4384, 4640, 5664)
D_IN = 6688


class Res:
    __slots__ = ("name", "w", "r", "dsem", "dcnt", "excl")

    def __init__(self, name, excl=False):
        self.name = name
        self.excl = excl
        self.w = None
        self.r = {}
        self.dsem = None
        self.dcnt = 0


class Sched:
    def __init__(self, nc):
        self.nc = nc
        self.eng = {"pe": nc.tensor, "act": nc.scalar, "dve": nc.vector, "pool": nc.gpsimd, "sp": nc.sync}
        self.sem = {k: nc.alloc_semaphore("c_" + k) for k in self.eng}
        self.cnt = {k: 0 for k in self.eng}
        self.seen = {k: {} for k in self.eng}
        self.dsems = []
        self.ninst = 0

    def _wait(self, e, deps):
        best = {}
        for d in deps:
            if d is None:
                continue
            s, v = d
            k = id(s)
            if k not in best or best[k][1] < v:
                best[k] = (s, v)
        for k, (s, v) in best.items():
            if e == "pe" and s is self.sem["pe"]:
                continue
            if self.seen[e].get(k, 0) >= v:
                continue
            self.eng[e].wait_ge(s, v)
            self.seen[e][k] = v

    def op(self, e, fn, reads=(), writes=(), inc=True):
        deps = []
        for r in reads:
            deps.append(r.w)
            if r.excl:
                deps.extend(tok for k, tok in r.r.items() if k != e)
        for w in writes:
            deps.append(w.w)
            deps.extend(w.r.values())
        self._wait(e, deps)
        ins = fn(self.eng[e])
        self.ninst += 1
        if inc:
            self.cnt[e] += 1
            ins.then_inc(self.sem[e], 1)
            tok = (self.sem[e], self.cnt[e])
        else:
            tok = (self.sem[e], self.cnt[e] + 1)
        for r in reads:
            r.r[e] = tok
        for w in writes:
            w.w = tok
            w.r = {}
        return ins

    def dma(self, q, out, in_, reads=(), writes=()):
        deps = []
        for r in reads:
            deps.append(r.w)
        for w in writes:
            deps.append(w.w)
            deps.extend(w.r.values())
        self._wait(q, deps)
        owner = writes[0] if writes else reads[0]
        if owner.dsem is None:
            owner.dsem = self.nc.alloc_semaphore("d_" + owner.name)
            self.dsems.append(owner)
        owner.dcnt += 16
        ins = self.eng[q].dma_start(out=out, in_=in_)
        ins.then_inc(owner.dsem, 16)
        self.ninst += 1
        tok = (owner.dsem, owner.dcnt)
        for r in reads:
            r.r["dma_" + q + owner.name] = tok
        for w in writes:
            w.w = tok
            w.r = {}
        return tok

    def barrier(self):
        toks = [(self.sem[k], self.cnt[k]) for k in self.eng if self.cnt[k] > 0]
        toks += [(o.dsem, o.dcnt) for o in self.dsems]
        for e in self.eng:
            self._wait(e, toks)


def build_program(n_groups=4, do_phase_b=True, stage=99):
    nc = bass.Bass("TRN2", target_bir_lowering=False)
    S = Sched(nc)

    def dram(name, shape, kind="ExternalInput"):
        return nc.dram_tensor(name, shape, F32, kind=kind).ap()

    xs = dram("xs", [SEQ, D])
    ropeC = dram("ropeC", [SEQ, 128])
    ropeS = dram("ropeS", [SEQ, 128])
    w_in = dram("w_in", [D, D_IN])
    w_rank = dram("w_rank", [D, 32])
    w_o = dram("w_o", [D, D])
    w_gu = dram("w_gu", [D, 2 * DFF])
    w_dn = dram("w_dn", [DFF, D])
    waug = dram("waug", [2, 33, 512])
    gmix = dram("gmix", [128, 8])
    gffn = dram("gffn", [128, 8])
    bgate = dram("bgate", [128, 16])
    gq_d = dram("gq", [128, 128])
    gk_d = dram("gk", [128, 128])
    gla_d = dram("gla", [128, 256])
    gfin_d = dram("gfin", [128, D])
    ident_d = dram("ident", [128, 128])
    tri_d = dram("tri", [128, 4, 128])
    mask_d = dram("mask", [128, 2, 128])
    y = dram("y", [OWN, D], kind="ExternalOutput")
    NBLK = 28
    wscr = nc.dram_tensor("wscr", [NBLK, 128, 4096], BF16).ap()

    _n = [0]
    stack_main = ExitStack()

    def sb(shape, dt, name, st=None):
        _n[0] += 1
        h = (st or stack_main).enter_context(nc.sbuf_tensor("%s_%d" % (name, _n[0]), shape, dt))
        return h.ap()

    def R(name):
        _n[0] += 1
        return Res("%s_%d" % (name, _n[0]))

    PS = [stack_main.enter_context(nc.psum_tensor("ps%d" % i, [128, 512], F32)).ap() for i in range(8)]
    PSB = [p.bitcast(BF16) for p in PS]
    PR = [R("ps%d" % i) for i in range(8)]
    for r_ in PR:
        r_.excl = True

    KT = sb([128, 2, SEQ], BF16, "KT")
    KTr = [R("KT") for _ in range(NT)]
    VV = sb([128, NT, 256], BF16, "VV")
    VVr = [R("VV") for _ in range(NT)]
    ON = sb([128, NTO, 1024], BF16, "ON")
    ONr = [R("ON") for _ in range(NTO)]

    ident = sb([128, 128], BF16, "ident")
    r_c = R("consts")
    c32 = sb([128, 128], BF16, "c32")
    ones = sb([128, 128], BF16, "ones")
    nsix = sb([128, 128], BF16, "nsix")
    gmix_t = sb([128, 8], F32, "gmix")
    gffn_t = sb([128, 8], F32, "gffn")
    bgate_t = sb([128, 16], F32, "bgate")
    gq_t = sb([128, 128], F32, "gq")
    gk_t = sb([128, 128], F32, "gk")
    gfin_t = sb([128, D], F32, "gfin")
    aug = sb([128, 128], BF16, "aug")
    r_aug = R("aug")
    state = sb([128, 4, 256], F32, "state")
    r_state = R("state")
    state_bf = sb([128, 4, 256], BF16, "state_bf")
    r_state_bf = R("state_bf")

    cres = []
    for dst, src, q in ((gmix_t, gmix, "sp"), (gffn_t, gffn, "sp"), (bgate_t, bgate, "sp"),
                        (gq_t, gq_d, "sp"), (gk_t, gk_d, "sp"), (gfin_t, gfin_d, "sp"),
                        (ident, ident_d, "pool")):
        rr = R("cst")
        S.dma(q, dst, src, writes=[rr])
        cres.append(rr)
    S.op("dve", lambda e: e.memset(ones, 1.0), writes=[r_c])
    S.op("dve", lambda e: e.memset(c32, 1.0 / 32.0), writes=[r_c])
    S.op("dve", lambda e: e.memset(nsix, -1.0 / 16.0), writes=[r_c])
    S.op("dve", lambda e: e.memset(aug, 0.0), writes=[r_aug])
    S.op("dve", lambda e: e.memset(aug[32:33, :], 1.0), reads=[r_aug], writes=[r_aug])
    S.op("dve", lambda e: e.memset(state, 0.0), writes=[r_state])
    S.op("dve", lambda e: e.memset(state_bf, 0.0), writes=[r_state_bf])
    S.op("dve", lambda e: e.tensor_scalar(out=gq_t, in0=gq_t, scalar1=float(128.0 ** -0.5), scalar2=None, op0=ALU.mult),
         reads=cres, writes=[r_c])
    r_cst = cres + [r_c]

    def mm(out, pairs, reads, wres, start=True, stop=True):
        n = len(pairs)
        for i, (l, r) in enumerate(pairs):
            S.op("pe", lambda e: e.matmul(out, lhsT=l, rhs=r, start=(start and i == 0), stop=(stop and i == n - 1)),
                 reads=reads, writes=[wres], inc=(i == n - 1))

    def act(out, in_, func, reads, writes, **kw):
        S.op("act", lambda e: e.activation(out=out, in_=in_, func=func, **kw), reads=reads, writes=writes)

    def tt(out, in0, in1, op, reads, writes, eng="dve"):
        S.op(eng, lambda e: e.tensor_tensor(out=out, in0=in0, in1=in1, op=op), reads=reads, writes=writes)

    def rstd_inplace(ss, r_ss, n):
        act(ss, ss, AF.Ln, [r_ss], [r_ss], scale=1.0 / n, bias=EPS)
        act(ss, ss, AF.Exp, [r_ss], [r_ss], scale=-0.5)

    scr = [sb([128, 512], F32, "scr%d" % i) for i in range(3)]
    scr_r = [R("scr%d" % i) for i in range(3)]
    junk = sb([128, 1024], BF16, "junk")
    r_junk = R("junk")
    ssq = sb([128, 8], F32, "ssq")
    r_ssq = R("ssq")
    xb = sb([128, 1024], BF16, "xb")
    r_xb = R("xb")

    def norm_transpose(x_ap, r_x, gcol, hT_out, r_out, pb):
        act(junk, x_ap, AF.Square, [r_x], [r_junk, r_ssq], accum_out=ssq[:, 0:1])
        rstd_inplace(ssq[:, 0:1], r_ssq, D)
        act(xb, x_ap, AF.Copy, [r_x, r_ssq], [r_xb], scale=ssq[:, 0:1])
        pv = PSB[pb].rearrange("p (c t) -> p c t", c=8)
        for c in range(8):
            S.op("pe", lambda e: e.transpose(out=pv[:, c, :], in_=xb[:, c * 128:(c + 1) * 128], identity=ident),
                 reads=[r_xb] + r_cst, writes=[PR[pb]], inc=(c == 7))
        tt(hT_out, pv, gcol.unsqueeze(2).broadcast_to([128, 8, 128]), ALU.mult, [PR[pb]] + r_cst, [r_out])

    def headnorm_rope(src_ps, r_src, nh, g_t, Ct, St, r_tab, out_bf, r_outbf):
        w = nh * 128
        a, b, c = scr[0][:, 0:w], scr[1][:, 0:w], scr[2][:, 0:w]
        ra, rb, rc = scr_r
        a3 = a.rearrange("p (h d) -> p h d", h=nh)
        b3 = b.rearrange("p (h d) -> p h d", h=nh)
        act(a, src_ps, AF.Square, [r_src], [ra])
        S.op("dve", lambda e: e.tensor_reduce(out=ssq[:, 0:nh], in_=a3, axis=AX.X, op=ALU.add), reads=[ra], writes=[r_ssq])
        rstd_inplace(ssq[:, 0:nh], r_ssq, 128)
        tt(a3, src_ps.rearrange("p (h d) -> p h d", h=nh), ssq[:, 0:nh].unsqueeze(2).broadcast_to([128, nh, 128]),
           ALU.mult, [r_src, r_ssq], [ra])
        tt(a3, a3, g_t.unsqueeze(1).broadcast_to([128, nh, 128]), ALU.mult, [ra] + r_cst, [ra])
        tt(b3, a3, Ct.unsqueeze(1).broadcast_to([128, nh, 128]), ALU.mult, [ra, r_tab], [rb])
        a4 = a.rearrange("p (h x f i) -> p h x f i", h=nh, x=2, f=2)
        c4 = c.rearrange("p (h x f i) -> p h x f i", h=nh, x=2, f=2)
        S4 = St.rearrange("p (x f i) -> p x f i", x=2, f=2)
        for f in range(2):
            tt(c4[:, :, :, f, :], a4[:, :, :, 1 - f, :], S4[:, :, f, :].unsqueeze(1).broadcast_to([128, nh, 2, 32]),
               ALU.mult, [ra, r_tab], [rc])
        tt(out_bf, b, c, ALU.add, [rb, rc], [r_outbf])

    w3 = lambda ap: ap.rearrange("(c p) n -> p c n", p=128)
    def blocks_for_group():
        bl = []
        wqb = [("wq%d" % n, w3(w_in[:, C_AQ + n * 512:C_AQ + (n + 1) * 512]), [128, 8, 512]) for n in range(2)]
        wgb = [("wg%d" % n, w3(w_in[:, C_GO + n * 512:C_GO + (n + 1) * 512]), [128, 8, 512]) for n in range(2)]
        wgb += [("wg%d" % (n + 2), w3(w_in[:, C_GA + n * 512:C_GA + (n + 1) * 512]), [128, 8, 512]) for n in range(4)]
        bl += [wqb[0], wgb[0], wgb[1], wqb[1], wgb[2], wgb[3], wgb[4], wgb[5]]
        for n in range(2):
            bl.append(("wo%d" % n, w3(w_o[:, n * 512:(n + 1) * 512]), [128, 8, 512]))
        for jb in range(6):
            j0 = jb * 4
            nj = min(4, NJ - j0)
            bl.append(("wgu_g%d" % jb, w3(w_gu[:, j0 * 128:(j0 + nj) * 128]), [128, 8, nj * 128]))
            bl.append(("wgu_u%d" % jb, w3(w_gu[:, DFF + j0 * 128:DFF + (j0 + nj) * 128]), [128, 8, nj * 128]))
        for n in range(2):
            for jb in range(3):
                j0 = jb * 8
                nj = min(8, NJ - j0)
                bl.append(("wd%d_%d" % (n, jb), w_dn[j0 * 128:(j0 + nj) * 128, n * 512:(n + 1) * 512].rearrange("(j p) n -> p j n", p=128),
                           [128, nj, 512]))
        return bl

    group_blocks = blocks_for_group()
    assert len(group_blocks) == NBLK
    conv_r = [R("conv%d" % b) for b in range(NBLK)]
    conv_state = {"next": 0}

    def blk_view(b):
        shape = group_blocks[b][2]
        return wscr[b][:, 0:shape[1] * shape[2]].rearrange("p (c n) -> p c n", c=shape[1])

    def emit_conversion(k=1):
        for _ in range(k):
            b = conv_state["next"]
            if b >= NBLK:
                return
            S.dma("pool", blk_view(b), group_blocks[b][1], writes=[conv_r[b]])
            conv_state["next"] += 1

    stA = ExitStack()
    tri = sb([128, 4, 128], BF16, "tri", stA)
    mask = sb([128, 2, 128], BF16, "mask", stA)
    gla_t = sb([128, 256], F32, "gla", stA)
    waug_t = sb([128, 2, 512], BF16, "waug", stA)
    r_waug = R("waug")
    S.op("dve", lambda e: e.memset(waug_t.rearrange("p a n -> p (a n)"), 0.0), writes=[r_waug])
    for dst, src, q in ((tri, tri_d, "pool"), (gla_t, gla_d, "sp"), (mask, mask_d, "pool")):
        rr = R("cst")
        S.dma(q, dst, src, writes=[rr])
        r_cst.append(rr)
    S.dma("pool", waug_t[0:33], waug.rearrange("a p n -> p a n"), writes=[r_waug])
    r_cst.append(r_waug)
    Wgla = sb([128, 8, 2048], BF16, "Wgla", stA)
    Wrank = sb([128, 8, 128], BF16, "Wrank", stA)
    Wkv = sb([128, 8, 512], BF16, "Wkv", stA)
    r_Wrank = R("Wrank")
    S.op("dve", lambda e: e.memset(Wrank.rearrange("p c n -> p (c n)"), 0.0), writes=[r_Wrank])
    w3 = lambda ap: ap.rearrange("(c p) n -> p c n", p=128)
    wres = []
    S.dma("pool", Wrank[:, :, 0:32], w3(w_rank), writes=[r_Wrank])
    wres.append(r_Wrank)
    for dst, src in ((Wkv, w3(w_in[:, C_AK:C_AK + 512])),
                     (Wgla[:, :, 512:1024], w3(w_in[:, C_GK:C_GK + 512])),
                     (Wgla[:, :, 1024:1536], w3(w_in[:, C_GV:C_GV + 512])),
                     (Wgla[:, :, 1536:2048], w3(w_in[:, C_GV + 512:C_GV + 1024])),
                     (Wgla[:, :, 0:512], w3(w_in[:, C_GQ:C_GQ + 512]))):
        rr = R("wA")
        S.dma("pool", dst, src, writes=[rr])
        wres.append(rr)
    r_WA = wres

    xt = [sb([128, 1024], F32, "xt%d" % i, stA) for i in range(2)]
    xt_r = [R("xt%d" % i) for i in range(2)]
    tabC = [sb([128, 128], F32, "tabC%d" % i, stA) for i in range(2)]
    tabS = [sb([128, 128], F32, "tabS%d" % i, stA) for i in range(2)]
    tab_r = [R("tab%d" % i) for i in range(2)]
    hT = sb([128, 8, 128], BF16, "hT", stA)
    r_hT = R("hT")
    k_sb = [sb([128, 512], BF16, "k_sb%d" % i, stA) for i in range(2)]
    q_sb = [sb([128, 512], BF16, "q_sb%d" % i, stA) for i in range(2)]
    vbf = [sb([128, 1024], BF16, "vbf%d" % i, stA) for i in range(2)]
    augs = [sb([128, 128], BF16, "aug%d" % i, stA) for i in range(2)]
    k_r = [R("k_sb%d" % i) for i in range(2)]
    q_r = [R("q_sb%d" % i) for i in range(2)]
    v_r = [R("vbf%d" % i) for i in range(2)]
    aug_r = [R("aug%d" % i) for i in range(2)]
    for i in range(2):
        S.op("dve", lambda e: e.memset(augs[i], 0.0), writes=[aug_r[i]])
        S.op("dve", lambda e: e.memset(augs[i][32:33, :], 1.0), reads=[aug_r[i]], writes=[aug_r[i]])
    SP = sb([128, 512], F32, "SP", stA)
    SPh = sb([128, 512], BF16, "SPh", stA)
    SPl = sb([128, 512], BF16, "SPl", stA)
    Ecum = sb([128, 512], F32, "Ecum", stA)
    Encum = sb([128, 512], F32, "Encum", stA)
    Erem = sb([128, 512], F32, "Erem", stA)
    Etot = sb([128, 4], F32, "Etot", stA)
    r_SP, r_SPh, r_SPl, r_Ecum, r_Encum, r_Erem, r_Etot = [R(n) for n in ("SP", "SPh", "SPl", "Ecum", "Encum", "Erem", "Etot")]
    qt = sb([128, 512], BF16, "qt", stA)
    kt_ = sb([128, 512], BF16, "kt", stA)
    kd = sb([128, 512], BF16, "kd", stA)
    r_qt, r_kt, r_kd = [R(n) for n in ("qt", "kt", "kd")]
    qT = sb([128, 4, 128], BF16, "qT", stA)
    kT = sb([128, 4, 128], BF16, "kT", stA)
    r_qT, r_kT = R("qT"), R("kT")
    Am = sb([128, 4, 128], BF16, "Am", stA)
    r_Am = R("Am")
    kbf = sb([128, 256], BF16, "kbf", stA)
    r_kbf = R("kbf")
    kraw = sb([128, 256], F32, "kraw", stA)
    r_kraw = R("kraw")
    ya = sb([128, 1024], BF16, "ya", stA)
    r_ya = R("ya")
    osum = sb([128, 1024], F32, "osum", stA)
    r_osum = R("osum")
    junkC = sb([128, 256], BF16, "junkC", stA)
    r_junkC = R("junkC")
    ssqC = sb([128, 4], F32, "ssqC", stA)
    r_ssqC = R("ssqC")


    def load_tile(t, slot, tables=True):
        S.dma("sp", xt[slot], xs[t * 128:(t + 1) * 128, :], writes=[xt_r[slot]])
        if tables:
            S.dma("sp", tabC[slot], ropeC[t * 128:(t + 1) * 128, :], writes=[tab_r[slot]])
            S.dma("sp", tabS[slot], ropeS[t * 128:(t + 1) * 128, :], writes=[tab_r[slot]])

    rot = [0]

    def pbank():
        b = rot[0] % 3
        rot[0] += 1
        return b

    junkN, r_junkN = junk, r_junk
    ssqN = sb([128, 1], F32, "ssqN", stA)
    r_ssqN = R("ssqN")
    xbA = sb([128, 1024], BF16, "xbA", stA)
    r_xbA = R("xbA")

    def stage_A0(c):
        s = c["slot"]
        act(junkN, xt[s], AF.Square, [xt_r[s]], [r_junkN, r_ssqN], accum_out=ssqN)
        rstd_inplace(ssqN, r_ssqN, D)
        act(xbA, xt[s], AF.Copy, [xt_r[s], r_ssqN], [r_xbA], scale=ssqN)

    def stage_A1(c):
        t, s, mode = c["t"], c["slot"], c["mode"]
        pv = PSB[3].rearrange("p (c t) -> p c t", c=8)
        for k in range(8):
            S.op("pe", lambda e: e.transpose(out=pv[:, k, :], in_=xbA[:, k * 128:(k + 1) * 128], identity=ident),
                 reads=[r_xbA] + r_cst, writes=[PR[3]], inc=(k == 7))
        tt(hT, pv, gmix_t.unsqueeze(2).broadcast_to([128, 8, 128]), ALU.mult, [PR[3]] + r_cst, [r_hT])
        hc = [hT[:, k, :] for k in range(8)]
        rd = [r_hT] + r_WA
        c["hc"], c["rd"] = hc, rd
        if mode != "P":
            b = pbank()
            mm(PS[b], [(hc[k], Wkv[:, k, :]) for k in range(8)], rd, PR[b])
            act(kraw, PS[b][:, 0:256], AF.Copy, [PR[b]], [r_kraw])
            act(VV[:, t, :], PS[b][:, 256:512], AF.Copy, [PR[b]], [VVr[t]])

    def stage_A1b(c):
        t, s, mode = c["t"], c["slot"], c["mode"]
        hc, rd = c["hc"], c["rd"]
        b = pbank()
        mm(PS[b][:, 0:128], [(Wrank[:, k, :], hc[k]) for k in range(8)], rd, PR[b])
        act(augs[s][0:32, :], PS[b][0:32, 0:128], AF.Copy, [PR[b]], [aug_r[s]])
        b = pbank()
        mm(PS[b], [(hc[k], Wgla[:, k, 512:1024]) for k in range(8)], rd, PR[b])
        act(k_sb[s], PS[b], AF.Copy, [PR[b]], [k_r[s]])
        if mode != "P":
            headnorm_rope(kraw, r_kraw, 2, gk_t, tabC[s], tabS[s], tab_r[s], kbf, r_kbf)

    def stage_A2(c):
        t, s, mode = c["t"], c["slot"], c["mode"]
        hc, rd = c["hc"], c["rd"]
        if mode != "state":
            b = pbank()
            mm(PS[b], [(hc[k], Wgla[:, k, 0:512]) for k in range(8)], rd, PR[b])
            S.op("dve", lambda e: e.tensor_copy(out=q_sb[s], in_=PS[b]), reads=[PR[b]], writes=[q_r[s]])
        b = pbank()
        mm(PS[b], [(hc[k], Wgla[:, k, 1024:1536]) for k in range(8)], rd, PR[b])
        act(vbf[s][:, 0:512], PS[b], AF.Copy, [PR[b]], [v_r[s]])

    def stage_A2b(c):
        t, s, mode = c["t"], c["slot"], c["mode"]
        hc, rd = c["hc"], c["rd"]
        b = pbank()
        mm(PS[b], [(hc[k], Wgla[:, k, 1536:2048]) for k in range(8)], rd, PR[b])
        act(vbf[s][:, 512:1024], PS[b], AF.Copy, [PR[b]], [v_r[s]])
        if mode != "P":
            for g in range(2):
                S.op("pe", lambda e: e.transpose(out=PSB[3][:, g * 128:(g + 1) * 128], in_=kbf[:, g * 128:(g + 1) * 128], identity=ident),
                     reads=[r_kbf] + r_cst, writes=[PR[3]], inc=(g == 1))
            S.op("dve", lambda e: e.tensor_copy(out=KT[:, :, t * 128:(t + 1) * 128],
                                                in_=PSB[3][:, 0:256].rearrange("p (g t) -> p g t", g=2)),
                 reads=[PR[3]], writes=[KTr[t]])

    def stage_B1(c):
        s, dirn = c["slot"], c["dirn"]
        mm(PS[4], [(augs[s], waug_t[:, dirn, :])], [aug_r[s]] + r_cst, PR[4])
        act(SP, PS[4], AF.Exp, [PR[4]], [r_SP], scale=-1.0)
        act(SP, SP, AF.Ln, [r_SP], [r_SP], bias=1.0)
        S.op("dve", lambda e: e.tensor_copy(out=SPh, in_=SP), reads=[r_SP], writes=[r_SPh])
        tt(SPl, SP, SPh, ALU.subtract, [r_SP, r_SPh], [r_SPl])

    def stage_B2(c):
        s, dirn, mode = c["slot"], c["dirn"], c["mode"]
        own = mode != "state"
        ci, ri = (0, 1) if dirn == 0 else (2, 3)
        rds = [r_SPh, r_SPl] + r_cst
        if own:
            mm(PS[5], [(tri[:, ci, :], SPh), (tri[:, ci, :], SPl)], rds, PR[5])
        mm(PS[6], [(tri[:, ri, :], SPh), (tri[:, ri, :], SPl)], rds, PR[6])
        for h in range(4):
            mm(PS[4][:, h * 128:(h + 1) * 128], [(SPh[:, h * 128:(h + 1) * 128], nsix), (SPl[:, h * 128:(h + 1) * 128], nsix)], rds, PR[4])
        if own:
            act(Ecum, PS[5], AF.Exp, [PR[5]], [r_Ecum])
            act(Encum, PS[5], AF.Exp, [PR[5]], [r_Encum], scale=-1.0)
        act(Erem, PS[6], AF.Exp, [PR[6]], [r_Erem])
        act(Etot, PS[4].rearrange("p (h t) -> p h t", h=4)[:, :, 0], AF.Exp, [PR[4]], [r_Etot])
        if own:
            S.op("dve", lambda e: e.scalar_tensor_tensor(out=qt, in0=q_sb[s], scalar=float(128.0 ** -0.5), in1=Ecum,
                                                         op0=ALU.mult, op1=ALU.mult), reads=[q_r[s], r_Ecum], writes=[r_qt])
            tt(kt_, k_sb[s], Encum, ALU.mult, [k_r[s], r_Encum], [r_kt])
        tt(kd, k_sb[s], Erem, ALU.mult, [k_r[s], r_Erem], [r_kd])

    def stage_C(c):
        t, s, dirn, mode = c["t"], c["slot"], c["dirn"], c["mode"]
        own = mode != "state"
        if own:
            for h in range(4):
                S.op("pe", lambda e: e.transpose(out=PSB[7][:, h * 128:(h + 1) * 128], in_=qt[:, h * 128:(h + 1) * 128], identity=ident),
                     reads=[r_qt] + r_cst, writes=[PR[7]], inc=False)
            for h in range(4):
                S.op("pe", lambda e: e.transpose(out=PSB[7][:, 512 + h * 128:512 + (h + 1) * 128], in_=kt_[:, h * 128:(h + 1) * 128], identity=ident),
                     reads=[r_kt] + r_cst, writes=[PR[7]], inc=(h == 3))
            S.op("dve", lambda e: e.tensor_copy(out=qT, in_=PSB[7][:, 0:512].rearrange("p (h t) -> p h t", h=4)),
                 reads=[PR[7]], writes=[r_qT])
            S.op("dve", lambda e: e.tensor_copy(out=kT, in_=PSB[7][:, 512:1024].rearrange("p (h t) -> p h t", h=4)),
                 reads=[PR[7]], writes=[r_kT])
            for h in range(4):
                S.op("pe", lambda e: e.matmul(PS[4][:, h * 128:(h + 1) * 128], lhsT=kT[:, h, :], rhs=qT[:, h, :], start=True, stop=True),
                     reads=[r_qT, r_kT], writes=[PR[4]], inc=(h == 3))
            tt(Am, PS[4].rearrange("p (h t) -> p h t", h=4), mask[:, dirn, :].unsqueeze(1).broadcast_to([128, 4, 128]),
               ALU.mult, [PR[4]] + r_cst, [r_Am])
        for h in range(4):
            bank = 7 if h < 2 else 4
            S.op("pe", lambda e: e.matmul(PS[bank][:, (h % 2) * 256:(h % 2 + 1) * 256], lhsT=kd[:, h * 128:(h + 1) * 128],
                                          rhs=vbf[s][:, h * 256:(h + 1) * 256], start=True, stop=True),
                 reads=[r_kd, v_r[s]], writes=[PR[bank]], inc=True)
        if own:
            for h in range(4):
                bank = 5 + h // 2
                o_ap = PS[bank][:, (h % 2) * 256:(h % 2 + 1) * 256]
                S.op("pe", lambda e: e.matmul(o_ap, lhsT=Am[:, h, :], rhs=vbf[s][:, h * 256:(h + 1) * 256], start=True, stop=False),
                     reads=[r_Am, v_r[s]], writes=[PR[bank]], inc=False)
                S.op("pe", lambda e: e.matmul(o_ap, lhsT=qT[:, h, :], rhs=state_bf[:, h, :], start=False, stop=True),
                     reads=[r_qT, r_state_bf], writes=[PR[bank]], inc=True)

    def stage_Cs(c):
        for h in range(4):
            bank = 7 if h < 2 else 4
            S.op("dve", lambda e: e.scalar_tensor_tensor(out=state[:, h, :], in0=state[:, h, :], scalar=Etot[:, h:h + 1],
                                                         in1=PS[bank][:, (h % 2) * 256:(h % 2 + 1) * 256],
                                                         op0=ALU.mult, op1=ALU.add),
                 reads=[r_state, r_Etot, PR[bank]], writes=[r_state])
        S.op("pool", lambda e: e.tensor_copy(out=state_bf.rearrange("p h v -> p (h v)"), in_=state.rearrange("p h v -> p (h v)")),
             reads=[r_state], writes=[r_state_bf])

    def stage_D(c):
        t, s, mode = c["t"], c["slot"], c["mode"]
        if mode == "N":
            act(ON[:, t, 0:512], PS[5], AF.Copy, [PR[5]], [ONr[t]])
            act(ON[:, t, 512:1024], PS[6], AF.Copy, [PR[6]], [ONr[t]])
        if mode == "P":
            tt(osum[:, 0:512], PS[5], ON[:, t, 0:512], ALU.add, [PR[5], ONr[t]], [r_osum])
            tt(osum[:, 512:1024], PS[6], ON[:, t, 512:1024], ALU.add, [PR[6], ONr[t]], [r_osum])
            for h in range(4):
                act(junkC, osum[:, h * 256:(h + 1) * 256], AF.Square, [r_osum], [r_junkC, r_ssqC], accum_out=ssqC[:, h:h + 1])
            rstd_inplace(ssqC, r_ssqC, 256)
            for h in range(4):
                S.op("dve", lambda e: e.scalar_tensor_tensor(out=ya[:, h * 256:(h + 1) * 256], in0=osum[:, h * 256:(h + 1) * 256],
                                                             scalar=ssqC[:, h:h + 1], in1=gla_t, op0=ALU.mult, op1=ALU.mult),
                     reads=[r_osum, r_ssqC] + r_cst, writes=[r_ya])

    def stage_D2(c):
        t, mode = c["t"], c["mode"]
        if mode == "P":
            pv = PSB[7].rearrange("p (c t) -> p c t", c=8)
            for k in range(8):
                S.op("pe", lambda e: e.transpose(out=pv[:, k, :], in_=ya[:, k * 128:(k + 1) * 128], identity=ident),
                     reads=[r_ya] + r_cst, writes=[PR[7]], inc=(k == 7))
            S.op("dve", lambda e: e.tensor_copy(out=ON[:, t, :].rearrange("p (c t) -> p c t", c=8), in_=pv),
                 reads=[PR[7]], writes=[ONr[t]])

    def run_sequence(tiles, dirn, mode_of, tables):
        ctxs = [{"t": t, "slot": i % 2, "dirn": dirn, "mode": mode_of(t)} for i, t in enumerate(tiles)]
        n = len(ctxs)
        if n == 0:
            return
        load_tile(tiles[0], 0, tables)
        if n > 1:
            load_tile(tiles[1], 1, tables)
        stage_A0(ctxs[0])
        stage_A1(ctxs[0])
        stage_A1b(ctxs[0])
        stage_A2(ctxs[0])
        stage_A2b(ctxs[0])
        if n > 1:
            stage_A0(ctxs[1])
        for i in range(n):
            emit_conversion(1)
            if i + 2 < n:
                load_tile(tiles[i + 2], i % 2, tables)
            if i + 1 < n:
                stage_A1(ctxs[i + 1])
            if i >= 1:
                stage_Cs(ctxs[i - 1])
                stage_D(ctxs[i - 1])
            stage_B1(ctxs[i])
            if i + 1 < n:
                stage_A1b(ctxs[i + 1])
            if i + 2 < n:
                stage_A0(ctxs[i + 2])
            if i + 1 < n:
                stage_A2(ctxs[i + 1])
            stage_B2(ctxs[i])
            if i + 1 < n:
                stage_A2b(ctxs[i + 1])
            if i >= 1:
                stage_D2(ctxs[i - 1])
            stage_C(ctxs[i])
        stage_Cs(ctxs[n - 1])
        stage_D(ctxs[n - 1])
        stage_D2(ctxs[n - 1])

    order = list(range(NT - 1, -1, -1))
    n3a = NTO
    if isinstance(stage, tuple):
        order = order[stage[0]:stage[1]]
        n3a = stage[2]
    run_sequence(order, 1, lambda t: "state" if t >= NTO else "N", True)
    S.op("dve", lambda e: e.memset(state, 0.0), reads=[r_state], writes=[r_state])
    S.op("dve", lambda e: e.memset(state_bf, 0.0), reads=[r_state_bf], writes=[r_state_bf])
    run_sequence(list(range(n3a)), 0, lambda t: "P", False)

    emit_conversion(NBLK)
    S.barrier()
    stA.close()

    stB = ExitStack()
    NRING = 4
    ring = [sb([128, 8, 512], BF16, "ring%d" % i, stB) for i in range(NRING)]
    ring_r = [R("ring%d" % i) for i in range(NRING)]
    xg = sb([128, 4, 1024], F32, "xg", stB)
    xg_r = [R("xg%d" % i) for i in range(4)]
    tC = [sb([128, 128], F32, "tC%d" % i, stB) for i in range(4)]
    tS = [sb([128, 128], F32, "tS%d" % i, stB) for i in range(4)]
    tCS_r = [R("tCS%d" % i) for i in range(4)]
    hTg = sb([128, 8, 512], BF16, "hTg", stB)
    hTg_r = [R("hTg%d" % i) for i in range(4)]
    big = sb([128, 32, 512], BF16, "big", stB)
    big_r = [R("big%d" % i) for i in range(32)]
    QT = big[:, 0:8, :]
    mixT = sb([128, 8, 512], BF16, "mixT", stB)
    mixT_r = [R("mixT%d" % i) for i in range(4)]
    xtmp = sb([128, 1024], F32, "xtmp", stB)
    r_xtmp = R("xtmp")
    qbfs = [sb([128, 512], BF16, "qbf%d" % i, stB) for i in range(2)]
    qbf_r = [R("qbf%d" % i) for i in range(2)]
    NPT = 6
    PT = [sb([128, 512], BF16, "PT%d" % i, stB) for i in range(NPT)]
    PT_r = [R("PT%d" % i) for i in range(NPT)]

    rden = sb([128, 512], F32, "rden", stB)
    r_rden = R("rden")
    ybt = sb([128, 512], F32, "ybt", stB)
    r_ybt = R("ybt")
    yat = sb([128, 512], F32, "yat", stB)
    r_yat = R("yat")
    sgt = sb([128, 512], F32, "sgt", stB)
    r_sgt = R("sgt")
    _sg16 = sgt.bitcast(BF16)
    dhi, dlo = _sg16[:, 0:512], _sg16[:, 512:1024]
    r_dhi = r_dlo = r_sgt

    all_blocks = []
    for g in range(n_groups):
        all_blocks += [(nm, b, shp) for b, (nm, src, shp) in enumerate(group_blocks)]
    stream = {"next_load": 0, "next_use": 0, "released": 0}

    def ring_view(i, shape):
        return ring[i % NRING][:, 0:shape[1], 0:shape[2]]

    def issue_loads():
        while stream["next_load"] < len(all_blocks) and stream["next_load"] < stream["released"] + NRING:
            i = stream["next_load"]
            name, b, shape = all_blocks[i]
            S.dma("pool", ring_view(i, shape), blk_view(b), reads=[conv_r[b]], writes=[ring_r[i % NRING]])
            stream["next_load"] += 1

    def next_block(expect):
        i = stream["next_use"]
        name, src, shape = all_blocks[i]
        assert name.startswith(expect), (name, expect)
        assert i < stream["next_load"], "block not loaded yet (ring too small)"
        stream["next_use"] += 1
        return ring_view(i, shape), ring_r[i % NRING]

    def release(n=1):
        stream["released"] += n
        issue_loads()

    if isinstance(stage, tuple) or stage < 5:
        do_phase_b = False
    if do_phase_b:
        issue_loads()

    out_toks = []
    for g in range(n_groups if do_phase_b else 0):
        T0 = g * 4
        for i in range(4):
            t = T0 + i
            S.dma("sp", xg[:, i, :], xs[t * 128:(t + 1) * 128, :], writes=[xg_r[i]])
        if g == 0:
            for i in range(4):
                t = T0 + i
                S.dma("sp", tC[i], ropeC[t * 128:(t + 1) * 128, :], writes=[tCS_r[i]])
                S.dma("sp", tS[i], ropeS[t * 128:(t + 1) * 128, :], writes=[tCS_r[i]])
            for i in range(4):
                norm_transpose(xg[:, i, :], xg_r[i], gmix_t, hTg[:, :, i * 128:(i + 1) * 128], hTg_r[i], 7)
        def gate_chunk(wg, wg_r, n, cc):
            ch = n * 4 + cc
            bank = 2 + (ch % 2)
            mm(PS[bank], [(wg[:, c, cc * 128:(cc + 1) * 128], hTg[:, c, :]) for c in range(8)], hTg_r + [wg_r], PR[bank])
            if ch < 8:
                act(big[:, 8 + ch, :], PS[bank], AF.Silu, [PR[bank]], [big_r[8 + ch]])
            else:
                act(big[:, 8 + ch, :], PS[bank], AF.Sigmoid, [PR[bank]] + r_cst, [big_r[8 + ch]], bias=bgate_t[:, ch - 8:ch - 7])

        for n in range(2):
            wqn, wqn_r = next_block("wq%d" % n)
            wga = [next_block("wg%d" % (2 * n)), next_block("wg%d" % (2 * n + 1))]
            def q_mm_hnr(i):
                mm(PS[i % 2], [(hTg[:, c, i * 128:(i + 1) * 128], wqn[:, c, :]) for c in range(8)], [hTg_r[i], wqn_r], PR[i % 2])
                headnorm_rope(PS[i % 2], PR[i % 2], 4, gq_t, tC[i], tS[i], tCS_r[i], qbfs[i % 2], qbf_r[i % 2])
            q_mm_hnr(0)
            for i in range(4):
                if i + 1 < 4:
                    q_mm_hnr(i + 1)
                for cc in range(2):
                    gate_chunk(wga[i // 2][0], wga[i // 2][1], 2 * n + i // 2, (i % 2) * 2 + cc)
                pv = PSB[7][:, 0:512].rearrange("p (c t) -> p c t", c=4)
                for c in range(4):
                    S.op("pe", lambda e: e.transpose(out=pv[:, c, :], in_=qbfs[i % 2][:, c * 128:(c + 1) * 128], identity=ident),
                         reads=[qbf_r[i % 2]] + r_cst, writes=[PR[7]], inc=(c == 3))
                S.op("dve", lambda e: e.tensor_copy(out=QT[:, 4 * n:4 * n + 4, i * 128:(i + 1) * 128], in_=pv), reads=[PR[7]], writes=big_r[4 * n:4 * n + 4])
            release(3)
        for n in range(4, 6):
            wg, wg_r = next_block("wg%d" % n)
            for cc in range(4):
                gate_chunk(wg, wg_r, n, cc)
            release(1)
        for c8 in range(8):
            tt(big[:, 8 + c8, :], big[:, 8 + c8, :], big[:, 16 + c8, :], ALU.mult, [big_r[8 + c8], big_r[16 + c8]], [big_r[8 + c8]], eng="pool")
        for i in range(4):
            t = T0 + i
            for kvg in range(2):
                accb = 2 + 2 * kvg
                rhs_q = QT[:, kvg * 4:(kvg + 1) * 4, i * 128:(i + 1) * 128]
                qres = big_r[kvg * 4:(kvg + 1) * 4]

                SB = (0, 1, 6, 7)

                def smm(kt):
                    b = SB[kt % 4]
                    S.op("pe", lambda e: e.matmul(PS[b].rearrange("p (h t) -> p h t", h=4), lhsT=KT[:, kvg, kt * 128:(kt + 1) * 128],
                                                  rhs=rhs_q, start=True, stop=True),
                         reads=[KTr[kt]] + qres, writes=[PR[b]], inc=True)
                for k0 in range(3):
                    smm(k0)
                for kt in range(NT):
                    if kt + 3 < NT:
                        smm(kt + 3)
                    p = PT[kt % NPT]
                    pr = PT_r[kt % NPT]
                    act(p, PS[SB[kt % 4]], AF.Exp, [PR[SB[kt % 4]]], [pr])
                    S.op("pe", lambda e: e.matmul(PS[accb], lhsT=VV[:, kt, kvg * 128:(kvg + 1) * 128], rhs=p, start=(kt == 0), stop=(kt == NT - 1)),
                         reads=[VVr[kt], pr], writes=[PR[accb]], inc=True)
                    if kt % 4 == 3:
                        for j in range(4):
                            kk = kt - 3 + j
                            S.op("pe", lambda e: e.matmul(PS[accb + 1][32 * j:32 * (j + 1), :], lhsT=ones[:, 0:32], rhs=PT[kk % NPT],
                                                          start=(kk < 4), stop=(kk >= NT - 4), tile_position=(0, 32 * j)),
                                 reads=[PT_r[kk % NPT]] + r_cst, writes=[PR[accb + 1]], inc=(j == 3))
                S.op("dve", lambda e: e.tensor_copy(out=dhi, in_=PS[accb + 1]), reads=[PR[accb + 1]], writes=[r_dhi])
                tt(dlo, PS[accb + 1], dhi, ALU.subtract, [PR[accb + 1], r_dhi], [r_dlo])
                mm(PS[accb + 1], [(c32, dhi), (c32, dlo)], [r_dhi, r_dlo] + r_cst, PR[accb + 1])
                S.op("dve", lambda e: e.reciprocal(out=rden, in_=PS[accb + 1]), reads=[PR[accb + 1]], writes=[r_rden])
                tt(ybt, PS[accb], rden, ALU.mult, [PR[accb], r_rden], [r_ybt])
                yb3 = ybt.rearrange("p (h t) -> p h t", h=4)
                ya3v = yat.rearrange("p (h t) -> p h t", h=4)
                gb = big[:, 24 + kvg * 4:24 + (kvg + 1) * 4, i * 128:(i + 1) * 128]
                ga = big[:, 8 + kvg * 4:8 + (kvg + 1) * 4, i * 128:(i + 1) * 128]
                tt(yb3, yb3, gb, ALU.mult, [r_ybt] + big_r[24 + kvg * 4:24 + (kvg + 1) * 4], [r_ybt])
                ya3 = ON[:, t, :].rearrange("p (c t) -> p c t", c=8)[:, kvg * 4:(kvg + 1) * 4, :]
                tt(ya3v, ya3, ga, ALU.mult, [ONr[t]] + big_r[8 + kvg * 4:8 + (kvg + 1) * 4], [r_yat], eng="pool")
                tt(mixT[:, kvg * 4:(kvg + 1) * 4, i * 128:(i + 1) * 128], yb3, ya3v, ALU.add, [r_ybt, r_yat], [mixT_r[i]])
        wo = [next_block("wo0"), next_block("wo1")]

        def wo_tile(i):
            for n in range(2):
                bank = 6 + n
                mm(PS[bank], [(mixT[:, c, i * 128:(i + 1) * 128], wo[n][0][:, c, :]) for c in range(8)], [mixT_r[i], wo[n][1]], PR[bank])
                tt(xg[:, i, n * 512:(n + 1) * 512], xg[:, i, n * 512:(n + 1) * 512], PS[bank], ALU.add, [xg_r[i], PR[bank]], [xg_r[i]])
        wo_tile(0)
        for i in range(4):
            if i + 1 < 4:
                wo_tile(i + 1)
            else:
                release(2)
            norm_transpose(xg[:, i, :], xg_r[i], gffn_t, hTg[:, :, i * 128:(i + 1) * 128], hTg_r[i], 0)
        for jb in range(6):
            j0 = jb * 4
            nj = min(4, NJ - j0)
            wgt, wgt_r = next_block("wgu_g%d" % jb)
            wup, wup_r = next_block("wgu_u%d" % jb)
            for jj in range(nj):
                j = j0 + jj
                bg, bu = (0, 1) if j % 2 == 0 else (2, 3)
                mm(PS[bg], [(wgt[:, c, jj * 128:(jj + 1) * 128], hTg[:, c, :]) for c in range(8)], hTg_r + [wgt_r], PR[bg])
                mm(PS[bu], [(wup[:, c, jj * 128:(jj + 1) * 128], hTg[:, c, :]) for c in range(8)], hTg_r + [wup_r], PR[bu])
                act(sgt, PS[bg], AF.Silu, [PR[bg]], [r_sgt])
                tt(big[:, j, :], sgt, PS[bu], ALU.mult, [r_sgt, PR[bu]], [big_r[j]])
            release(2)
        for n in range(2):
            for jb in range(3):
                j0 = jb * 8
                nj = min(8, NJ - j0)
                wd, wd_r = next_block("wd%d_%d" % (n, jb))
                for i in range(4):
                    bank = 4 + i
                    for jj in range(nj):
                        j = j0 + jj
                        S.op("pe", lambda e: e.matmul(PS[bank], lhsT=big[:, j, i * 128:(i + 1) * 128], rhs=wd[:, jj, :],
                                                      start=(j == 0), stop=(j == NJ - 1)),
                             reads=[big_r[j], wd_r], writes=[PR[bank]], inc=(jj == nj - 1))
                release(1)
                if g + 1 < n_groups and (n, jb) in ((0, 0), (0, 1), (0, 2), (1, 0)):
                    i2 = {(0, 0): 0, (0, 1): 1, (0, 2): 2, (1, 0): 3}[(n, jb)]
                    t2 = (g + 1) * 4 + i2
                    S.dma("sp", xtmp, xs[t2 * 128:(t2 + 1) * 128, :], writes=[r_xtmp])
                    S.dma("sp", tC[i2], ropeC[t2 * 128:(t2 + 1) * 128, :], writes=[tCS_r[i2]])
                    S.dma("sp", tS[i2], ropeS[t2 * 128:(t2 + 1) * 128, :], writes=[tCS_r[i2]])
                    norm_transpose(xtmp, r_xtmp, gmix_t, hTg[:, :, i2 * 128:(i2 + 1) * 128], hTg_r[i2], 0)
            for i in range(4):
                bank = 4 + i
                tt(xg[:, i, n * 512:(n + 1) * 512], xg[:, i, n * 512:(n + 1) * 512], PS[bank], ALU.add, [xg_r[i], PR[bank]], [xg_r[i]])
        for i in range(4):
            t = T0 + i
            act(junk, xg[:, i, :], AF.Square, [xg_r[i]], [r_junk, r_ssq], accum_out=ssq[:, 0:1])
            rstd_inplace(ssq[:, 0:1], r_ssq, D)
            S.op("dve", lambda e: e.scalar_tensor_tensor(out=xg[:, i, :], in0=xg[:, i, :], scalar=ssq[:, 0:1], in1=gfin_t, op0=ALU.mult, op1=ALU.mult),
                 reads=[xg_r[i], r_ssq] + r_cst, writes=[xg_r[i]])
            out_toks.append(S.dma("sp", y[t * 128:(t + 1) * 128, :], xg[:, i, :], reads=[xg_r[i]]))

    S._wait("sp", out_toks)
    S.barrier()
    stB.close()
    stack_main.close()
    return nc, S


def _rope_tables():
    rows = SEQ // GRID_W
    row = np.repeat(np.arange(rows), GRID_W).astype(np.float32)
    col = np.tile(np.arange(GRID_W), rows).astype(np.float32)
    freqs = (np.float32(10000.0) ** (-np.arange(0, 64, 2, dtype=np.float32) / np.float32(64))).astype(np.float32)
    ar = row[:, None] * freqs[None, :]
    ac = col[:, None] * freqs[None, :]
    cr, sr, cc, sc = np.cos(ar), np.sin(ar), np.cos(ac), np.sin(ac)
    Ct = np.concatenate([cr, cr, cc, cc], axis=1).astype(np.float32)
    St = np.concatenate([-sr, sr, -sc, sc], axis=1).astype(np.float32)
    return Ct, St


def _consts():
    m = np.arange(128)[:, None]
    i = np.arange(128)[None, :]
    s = np.float32(-1.0 / 16.0)
    tri = np.stack([(m <= i), (m > i), (m >= i), (m < i)], axis=1).astype(np.float32) * s
    mask = np.stack([(m <= i), (m > i)], axis=1).astype(np.float32)
    return np.ascontiguousarray(tri), np.ascontiguousarray(mask), np.eye(128, dtype=np.float32)


_PROGRAM = {}


def _make_in_maps(x, mix_norm, w_in, w_gk_fwd, b_gk_fwd, w_gk_bwd, b_gk_bwd, gla_norm,
                  q_norm, k_norm, b_gate, w_o, ffn_norm, w_gate_up, w_down, final_norm):
    f = lambda a: np.ascontiguousarray(np.asarray(a, dtype=np.float32))
    x = f(x)
    w_in0 = f(w_in)[0]
    Ct, St = _rope_tables()
    tri, mask, ident = _consts()
    col = lambda v: np.ascontiguousarray(f(v).reshape(-1, 128).T)
    bc = lambda v: np.ascontiguousarray(np.broadcast_to(f(v).reshape(1, -1), (128, f(v).size)))
    zeros16 = np.zeros((16, 512), np.float32)
    common = {
        "w_in": w_in0, "w_o": f(w_o)[0], "w_gu": f(w_gate_up)[0], "w_dn": f(w_down)[0],
        "gmix": col(mix_norm[0]), "gffn": col(ffn_norm[0]), "bgate": col(np.asarray(b_gate)[0].reshape(-1)),
        "gq": bc(q_norm[0]), "gk": bc(k_norm[0]), "gla": bc(gla_norm[0]), "gfin": bc(final_norm),
        "ident": ident, "tri": tri, "mask": mask,
    }
    wf, bf_, wb, bb = f(w_gk_fwd)[0], f(b_gk_fwd)[0], f(w_gk_bwd)[0], f(b_gk_bwd)[0]
    in_maps = []
    for c in range(8):
        b, half = c // 2, c % 2
        m = dict(common)
        if half == 0:
            m["xs"] = x[b]
            m["ropeC"], m["ropeS"] = Ct, St
            wP, bP, wN, bN = wf, bf_, wb, bb
            m["w_rank"] = np.ascontiguousarray(w_in0[:, C_RF:C_RF + 32])
        else:
            m["xs"] = np.ascontiguousarray(x[b, ::-1, :])
            m["ropeC"], m["ropeS"] = np.ascontiguousarray(Ct[::-1]), np.ascontiguousarray(St[::-1])
            wP, bP, wN, bN = wb, bb, wf, bf_
            m["w_rank"] = np.ascontiguousarray(np.concatenate([w_in0[:, C_RB:C_RB + 16], w_in0[:, C_RF:C_RF + 16]], axis=1))
        augP = np.concatenate([wP, zeros16, bP[None, :]], axis=0)
        augN = np.concatenate([zeros16, wN, bN[None, :]], axis=0)
        m["waug"] = np.ascontiguousarray(np.stack([augP, augN], axis=0))
        in_maps.append(m)
    return in_maps


def kernel(**inputs):
    if "nc" not in _PROGRAM:
        _PROGRAM["nc"], _ = build_program()
    nc = _PROGRAM["nc"]
    in_maps = _make_in_maps(**inputs)
    res = run_bass_kernel_spmd(nc, in_maps, core_ids=list(range(8)))
    out = np.empty((4, SEQ, D), np.float32)
    for c in range(8):
        b, half = c // 2, c % 2
        yc = np.asarray(res.results[c]["y"], dtype=np.float32)
        if half == 0:
            out[b, :OWN] = yc
        else:
            out[b, OWN:] = yc[::-1]
    return out
```

```python
from contextlib import ExitStack
import numpy as np
import concourse.bass as bass
import concourse.mybir as mybir
from concourse.bass_utils import run_bass_kernel_spmd

F32 = mybir.dt.float32
BF16 = mybir.dt.bfloat16
AF = mybir.ActivationFunctionType
ALU = mybir.AluOpType
AX = mybir.AxisListType

D = 1024
SEQ = 4096
OWN = 2048
NT = 32
NTO = 16
DFF = 2816
NJ = 22
EPS = 1e-6
GRID_W = 64
C_GQ, C_GK, C_GV, C_GO, C_RF, C_RB, C_AQ, C_AK, C_AV, C_GA, C_GB = (
    0, 512, 1024, 2048, 3072, 3088, 3104, 4128, 4384, 4640, 5664)
D_IN = 6688


class Res:
    __slots__ = ("name", "w", "r", "dsem", "dcnt", "excl")

    def __init__(self, name, excl=False):
        self.name = name
        self.excl = excl
        self.w = None
        self.r = {}
        self.dsem = None
        self.dcnt = 0


class Sched:
    def __init__(self, nc):
        self.nc = nc
        self.eng = {"pe": nc.tensor, "act": nc.scalar, "dve": nc.vector, "pool": nc.gpsimd, "sp": nc.sync}
        self.sem = {k: nc.alloc_semaphore("c_" + k) for k in self.eng}
        self.cnt = {k: 0 for k in self.eng}
        self.seen = {k: {} for k in self.eng}
        self.dsems = []
        self.ninst = 0

    def _wait(self, e, deps):
        best = {}
        for d in deps:
            if d is None:
                continue
            s, v = d
            k = id(s)
            if k not in best or best[k][1] < v:
                best[k] = (s, v)
        for k, (s, v) in best.items():
            if e == "pe" and s is self.sem["pe"]:
                continue
            if self.seen[e].get(k, 0) >= v:
                continue
            self.eng[e].wait_ge(s, v)
            self.seen[e][k] = v

    def op(self, e, fn, reads=(), writes=(), inc=True):
        deps = []
        for r in reads:
            deps.append(r.w)
            if r.excl:
                deps.extend(tok for k, tok in r.r.items() if k != e)
        for w in writes:
            deps.append(w.w)
            deps.extend(w.r.values())
        self._wait(e, deps)
        ins = fn(self.eng[e])
        self.ninst += 1
        if inc:
            self.cnt[e] += 1
            ins.then_inc(self.sem[e], 1)
            tok = (self.sem[e], self.cnt[e])
        else:
            tok = (self.sem[e], self.cnt[e] + 1)
        for r in reads:
            r.r[e] = tok
        for w in writes:
            w.w = tok
            w.r = {}
        return ins

    def dma(self, q, out, in_, reads=(), writes=()):
        deps = []
        for r in reads:
            deps.append(r.w)
        for w in writes:
            deps.append(w.w)
            deps.extend(w.r.values())
        self._wait(q, deps)
        owner = writes[0] if writes else reads[0]
        if owner.dsem is None:
            owner.dsem = self.nc.alloc_semaphore("d_" + owner.name)
            self.dsems.append(owner)
        owner.dcnt += 16
        ins = self.eng[q].dma_start(out=out, in_=in_)
        ins.then_inc(owner.dsem, 16)
        self.ninst += 1
        tok = (owner.dsem, owner.dcnt)
        for r in reads:
            r.r["dma_" + q + owner.name] = tok
        for w in writes:
            w.w = tok
            w.r = {}
        return tok

    def barrier(self):
        toks = [(self.sem[k], self.cnt[k]) for k in self.eng if self.cnt[k] > 0]
        toks += [(o.dsem, o.dcnt) for o in self.dsems]
        for e in self.eng:
            self._wait(e, toks)


def build_program(n_groups=4, do_phase_b=True, stage=99):
    nc = bass.Bass("TRN2", target_bir_lowering=False)
    S = Sched(nc)

    def dram(name, shape, kind="ExternalInput"):
        return nc.dram_tensor(name, shape, F32, kind=kind).ap()

    xs = dram("xs", [SEQ, D])
    ropeC = dram("ropeC", [SEQ, 128])
    ropeS = dram("ropeS", [SEQ, 128])
    w_in = dram("w_in", [D, D_IN])
    w_rank = dram("w_rank", [D, 32])
    w_o = dram("w_o", [D, D])
    w_gu = dram("w_gu", [D, 2 * DFF])
    w_dn = dram("w_dn", [DFF, D])
    waug = dram("waug", [2, 33, 512])
    gmix = dram("gmix", [128, 8])
    gffn = dram("gffn", [128, 8])
    bgate = dram("bgate", [128, 16])
    gq_d = dram("gq", [128, 128])
    gk_d = dram("gk", [128, 128])
    gla_d = dram("gla", [128, 256])
    gfin_d = dram("gfin", [128, D])
    ident_d = dram("ident", [128, 128])
    tri_d = dram("tri", [128, 4, 128])
    mask_d = dram("mask", [128, 2, 128])
    y = dram("y", [OWN, D], kind="ExternalOutput")
    NBLK = 28
    wscr = nc.dram_tensor("wscr", [NBLK, 128, 4096], BF16).ap()

    _n = [0]
    stack_main = ExitStack()

    def sb(shape, dt, name, st=None):
        _n[0] += 1
        h = (st or stack_main).enter_context(nc.sbuf_tensor("%s_%d" % (name, _n[0]), shape, dt))
        return h.ap()

    def R(name):
        _n[0] += 1
        return Res("%s_%d" % (name, _n[0]))

    PS = [stack_main.enter_context(nc.psum_tensor("ps%d" % i, [128, 512], F32)).ap() for i in range(8)]
    PSB = [p.bitcast(BF16) for p in PS]
    PR = [R("ps%d" % i) for i in range(8)]
    for r_ in PR:
        r_.excl = True

    KT = sb([128, 2, SEQ], BF16, "KT")
    KTr = [R("KT") for _ in range(NT)]
    VV = sb([128, NT, 256], BF16, "VV")
    VVr = [R("VV") for _ in range(NT)]
    ON = sb([128, NTO, 1024], BF16, "ON")
    ONr = [R("ON") for _ in range(NTO)]

    ident = sb([128, 128], BF16, "ident")
    r_c = R("consts")
    c32 = sb([128, 128], BF16, "c32")
    ones = sb([128, 128], BF16, "ones")
    nsix = sb([128, 128], BF16, "nsix")
    gmix_t = sb([128, 8], F32, "gmix")
    gffn_t = sb([128, 8], F32, "gffn")
    bgate_t = sb([128, 16], F32, "bgate")
    gq_t = sb([128, 128], F32, "gq")
    gk_t = sb([128, 128], F32, "gk")
    gfin_t = sb([128, D], F32, "gfin")
    aug = sb([128, 128], BF16, "aug")
    r_aug = R("aug")
    state = sb([128, 4, 256], F32, "state")
    r_state = R("state")
    state_bf = sb([128, 4, 256], BF16, "state_bf")
    r_state_bf = R("state_bf")

    cres = []
    for dst, src, q in ((gmix_t, gmix, "sp"), (gffn_t, gffn, "sp"), (bgate_t, bgate, "sp"),
                        (gq_t, gq_d, "sp"), (gk_t, gk_d, "sp"), (gfin_t, gfin_d, "sp"),
                        (ident, ident_d, "pool")):
        rr = R("cst")
        S.dma(q, dst, src, writes=[rr])
        cres.append(rr)
    S.op("dve", lambda e: e.memset(ones, 1.0), writes=[r_c])
    S.op("dve", lambda e: e.memset(c32, 1.0 / 32.0), writes=[r_c])
    S.op("dve", lambda e: e.memset(nsix, -1.0 / 16.0), writes=[r_c])
    S.op("dve", lambda e: e.memset(aug, 0.0), writes=[r_aug])
    S.op("dve", lambda e: e.memset(aug[32:33, :], 1.0), reads=[r_aug], writes=[r_aug])
    S.op("dve", lambda e: e.memset(state, 0.0), writes=[r_state])
    S.op("dve", lambda e: e.memset(state_bf, 0.0), writes=[r_state_bf])
    S.op("dve", lambda e: e.tensor_scalar(out=gq_t, in0=gq_t, scalar1=float(128.0 ** -0.5), scalar2=None, op0=ALU.mult),
         reads=cres, writes=[r_c])
    r_cst = cres + [r_c]

    def mm(out, pairs, reads, wres, start=True, stop=True):
        n = len(pairs)
        for i, (l, r) in enumerate(pairs):
            S.op("pe", lambda e: e.matmul(out, lhsT=l, rhs=r, start=(start and i == 0), stop=(stop and i == n - 1)),
                 reads=reads, writes=[wres], inc=(i == n - 1))

    def act(out, in_, func, reads, writes, **kw):
        S.op("act", lambda e: e.activation(out=out, in_=in_, func=func, **kw), reads=reads, writes=writes)

    def tt(out, in0, in1, op, reads, writes, eng="dve"):
        S.op(eng, lambda e: e.tensor_tensor(out=out, in0=in0, in1=in1, op=op), reads=reads, writes=writes)

    def rstd_inplace(ss, r_ss, n):
        act(ss, ss, AF.Ln, [r_ss], [r_ss], scale=1.0 / n, bias=EPS)
        act(ss, ss, AF.Exp, [r_ss], [r_ss], scale=-0.5)

    scr = [sb([128, 512], F32, "scr%d" % i) for i in range(3)]
    scr_r = [R("scr%d" % i) for i in range(3)]
    junk = sb([128, 1024], BF16, "junk")
    r_junk = R("junk")
    ssq = sb([128, 8], F32, "ssq")
    r_ssq = R("ssq")
    xb = sb([128, 1024], BF16, "xb")
    r_xb = R("xb")

    def norm_transpose(x_ap, r_x, gcol, hT_out, r_out, pb):
        act(junk, x_ap, AF.Square, [r_x], [r_junk, r_ssq], accum_out=ssq[:, 0:1])
        rstd_inplace(ssq[:, 0:1], r_ssq, D)
        act(xb, x_ap, AF.Copy, [r_x, r_ssq], [r_xb], scale=ssq[:, 0:1])
        pv = PSB[pb].rearrange("p (c t) -> p c t", c=8)
        for c in range(8):
            S.op("pe", lambda e: e.transpose(out=pv[:, c, :], in_=xb[:, c * 128:(c + 1) * 128], identity=ident),
                 reads=[r_xb] + r_cst, writes=[PR[pb]], inc=(c == 7))
        tt(hT_out, pv, gcol.unsqueeze(2).broadcast_to([128, 8, 128]), ALU.mult, [PR[pb]] + r_cst, [r_out])

    def headnorm_rope(src_ps, r_src, nh, g_t, Ct, St, r_tab, out_bf, r_outbf):
        w = nh * 128
        a, b, c = scr[0][:, 0:w], scr[1][:, 0:w], scr[2][:, 0:w]
        ra, rb, rc = scr_r
        a3 = a.rearrange("p (h d) -> p h d", h=nh)
        b3 = b.rearrange("p (h d) -> p h d", h=nh)
        act(a, src_ps, AF.Square, [r_src], [ra])
        S.op("dve", lambda e: e.tensor_reduce(out=ssq[:, 0:nh], in_=a3, axis=AX.X, op=ALU.add), reads=[ra], writes=[r_ssq])
        rstd_inplace(ssq[:, 0:nh], r_ssq, 128)
        tt(a3, src_ps.rearrange("p (h d) -> p h d", h=nh), ssq[:, 0:nh].unsqueeze(2).broadcast_to([128, nh, 128]),
           ALU.mult, [r_src, r_ssq], [ra])
        tt(a3, a3, g_t.unsqueeze(1).broadcast_to([128, nh, 128]), ALU.mult, [ra] + r_cst, [ra])
        tt(b3, a3, Ct.unsqueeze(1).broadcast_to([128, nh, 128]), ALU.mult, [ra, r_tab], [rb])
        a4 = a.rearrange("p (h x f i) -> p h x f i", h=nh, x=2, f=2)
        c4 = c.rearrange("p (h x f i) -> p h x f i", h=nh, x=2, f=2)
        S4 = St.rearrange("p (x f i) -> p x f i", x=2, f=2)
        for f in range(2):
            tt(c4[:, :, :, f, :], a4[:, :, :, 1 - f, :], S4[:, :, f, :].unsqueeze(1).broadcast_to([128, nh, 2, 32]),
               ALU.mult, [ra, r_tab], [rc])
        tt(out_bf, b, c, ALU.add, [rb, rc], [r_outbf])

    w3 = lambda ap: ap.rearrange("(c p) n -> p c n", p=128)
    def blocks_for_group():
        bl = []
        wqb = [("wq%d" % n, w3(w_in[:, C_AQ + n * 512:C_AQ + (n + 1) * 512]), [128, 8, 512]) for n in range(2)]
        wgb = [("wg%d" % n, w3(w_in[:, C_GO + n * 512:C_GO + (n + 1) * 512]), [128, 8, 512]) for n in range(2)]
        wgb += [("wg%d" % (n + 2), w3(w_in[:, C_GA + n * 512:C_GA + (n + 1) * 512]), [128, 8, 512]) for n in range(4)]
        bl += [wqb[0], wgb[0], wgb[1], wqb[1], wgb[2], wgb[3], wgb[4], wgb[5]]
        for n in range(2):
            bl.append(("wo%d" % n, w3(w_o[:, n * 512:(n + 1) * 512]), [128, 8, 512]))
        for jb in range(6):
            j0 = jb * 4
            nj = min(4, NJ - j0)
            bl.append(("wgu_g%d" % jb, w3(w_gu[:, j0 * 128:(j0 + nj) * 128]), [128, 8, nj * 128]))
            bl.append(("wgu_u%d" % jb, w3(w_gu[:, DFF + j0 * 128:DFF + (j0 + nj) * 128]), [128, 8, nj * 128]))
        for n in range(2):
            for jb in range(3):
                j0 = jb * 8
                nj = min(8, NJ - j0)
                bl.append(("wd%d_%d" % (n, jb), w_dn[j0 * 128:(j0 + nj) * 128, n * 512:(n + 1) * 512].rearrange("(j p) n -> p j n", p=128),
                           [128, nj, 512]))
        return bl

    group_blocks = blocks_for_group()
    assert len(group_blocks) == NBLK
    conv_r = [R("conv%d" % b) for b in range(NBLK)]
    conv_state = {"next": 0}

    def blk_view(b):
        shape = group_blocks[b][2]
        return wscr[b][:, 0:shape[1] * shape[2]].rearrange("p (c n) -> p c n", c=shape[1])

    def emit_conversion(k=1):
        for _ in range(k):
            b = conv_state["next"]
            if b >= NBLK:
                return
            S.dma("pool", blk_view(b), group_blocks[b][1], writes=[conv_r[b]])
            conv_state["next"] += 1

    stA = ExitStack()
    tri = sb([128, 4, 128], BF16, "tri", stA)
    mask = sb([128, 2, 128], BF16, "mask", stA)
    gla_t = sb([128, 256], F32, "gla", stA)
    waug_t = sb([128, 2, 512], BF16, "waug", stA)
    r_waug = R("waug")
    S.op("dve", lambda e: e.memset(waug_t.rearrange("p a n -> p (a n)"), 0.0), writes=[r_waug])
    for dst, src, q in ((tri, tri_d, "pool"), (gla_t, gla_d, "sp"), (mask, mask_d, "pool")):
        rr = R("cst")
        S.dma(q, dst, src, writes=[rr])
        r_cst.append(rr)
    S.dma("pool", waug_t[0:33], waug.rearrange("a p n -> p a n"), writes=[r_waug])
    r_cst.append(r_waug)
    Wgla = sb([128, 8, 2048], BF16, "Wgla", stA)
    Wrank = sb([128, 8, 128], BF16, "Wrank", stA)
    Wkv = sb([128, 8, 512], BF16, "Wkv", stA)
    r_Wrank = R("Wrank")
    S.op("dve", lambda e: e.memset(Wrank.rearrange("p c n -> p (c n)"), 0.0), writes=[r_Wrank])
    w3 = lambda ap: ap.rearrange("(c p) n -> p c n", p=128)
    wres = []
    S.dma("pool", Wrank[:, :, 0:32], w3(w_rank), writes=[r_Wrank])
    wres.append(r_Wrank)
    for dst, src in ((Wkv, w3(w_in[:, C_AK:C_AK + 512])),
                     (Wgla[:, :, 512:1024], w3(w_in[:, C_GK:C_GK + 512])),
                     (Wgla[:, :, 1024:1536], w3(w_in[:, C_GV:C_GV + 512])),
                     (Wgla[:, :, 1536:2048], w3(w_in[:, C_GV + 512:C_GV + 1024])),
                     (Wgla[:, :, 0:512], w3(w_in[:, C_GQ:C_GQ + 512]))):
        rr = R("wA")
        S.dma("pool", dst, src, writes=[rr])
        wres.append(rr)
    r_WA = wres

    xt = [sb([128, 1024], F32, "xt%d" % i, stA) for i in range(2)]
    xt_r = [R("xt%d" % i) for i in range(2)]
    tabC = [sb([128, 128], F32, "tabC%d" % i, stA) for i in range(2)]
    tabS = [sb([128, 128], F32, "tabS%d" % i, stA) for i in range(2)]
    tab_r = [R("tab%d" % i) for i in range(2)]
    hT = sb([128, 8, 128], BF16, "hT", stA)
    r_hT = R("hT")
    k_sb = [sb([128, 512], BF16, "k_sb%d" % i, stA) for i in range(2)]
    q_sb = [sb([128, 512], BF16, "q_sb%d" % i, stA) for i in range(2)]
    vbf = [sb([128, 1024], BF16, "vbf%d" % i, stA) for i in range(2)]
    augs = [sb([128, 128], BF16, "aug%d" % i, stA) for i in range(2)]
    k_r = [R("k_sb%d" % i) for i in range(2)]
    q_r = [R("q_sb%d" % i) for i in range(2)]
    v_r = [R("vbf%d" % i) for i in range(2)]
    aug_r = [R("aug%d" % i) for i in range(2)]
    for i in range(2):
        S.op("dve", lambda e: e.memset(augs[i], 0.0), writes=[aug_r[i]])
        S.op("dve", lambda e: e.memset(augs[i][32:33, :], 1.0), reads=[aug_r[i]], writes=[aug_r[i]])
    SP = sb([128, 512], F32, "SP", stA)
    SPh = sb([128, 512], BF16, "SPh", stA)
    SPl = sb([128, 512], BF16, "SPl", stA)
    Ecum = sb([128, 512], F32, "Ecum", stA)
    Encum = sb([128, 512], F32, "Encum", stA)
    Erem = sb([128, 512], F32, "Erem", stA)
    Etot = sb([128, 4], F32, "Etot", stA)
    r_SP, r_SPh, r_SPl, r_Ecum, r_Encum, r_Erem, r_Etot = [R(n) for n in ("SP", "SPh", "SPl", "Ecum", "Encum", "Erem", "Etot")]
    qt = sb([128, 512], BF16, "qt", stA)
    kt_ = sb([128, 512], BF16, "kt", stA)
    kd = sb([128, 512], BF16, "kd", stA)
    r_qt, r_kt, r_kd = [R(n) for n in ("qt", "kt", "kd")]
    qT = sb([128, 4, 128], BF16, "qT", stA)
    kT = sb([128, 4, 128], BF16, "kT", stA)
    r_qT, r_kT = R("qT"), R("kT")
    Am = sb([128, 4, 128], BF16, "Am", stA)
    r_Am = R("Am")
    kbf = sb([128, 256], BF16, "kbf", stA)
    r_kbf = R("kbf")
    kraw = sb([128, 256], F32, "kraw", stA)
    r_kraw = R("kraw")
    ya = sb([128, 1024], BF16, "ya", stA)
    r_ya = R("ya")
    osum = sb([128, 1024], F32, "osum", stA)
    r_osum = R("osum")
    junkC = sb([128, 256], BF16, "junkC", stA)
    r_junkC = R("junkC")
    ssqC = sb([128, 4], F32, "ssqC", stA)
    r_ssqC = R("ssqC")


    def load_tile(t, slot, tables=True):
        S.dma("sp", xt[slot], xs[t * 128:(t + 1) * 128, :], writes=[xt_r[slot]])
        if tables:
            S.dma("sp", tabC[slot], ropeC[t * 128:(t + 1) * 128, :], writes=[tab_r[slot]])
            S.dma("sp", tabS[slot], ropeS[t * 128:(t + 1) * 128, :], writes=[tab_r[slot]])

    rot = [0]

    def pbank():
        b = rot[0] % 3
        rot[0] += 1
        return b

    junkN, r_junkN = junk, r_junk
    ssqN = sb([128, 1], F32, "ssqN", stA)
    r_ssqN = R("ssqN")
    xbA = sb([128, 1024], BF16, "xbA", stA)
    r_xbA = R("xbA")

    def stage_A0(c):
        s = c["slot"]
        act(junkN, xt[s], AF.Square, [xt_r[s]], [r_junkN, r_ssqN], accum_out=ssqN)
        rstd_inplace(ssqN, r_ssqN, D)
        act(xbA, xt[s], AF.Copy, [xt_r[s], r_ssqN], [r_xbA], scale=ssqN)

    def stage_A1(c):
        t, s, mode = c["t"], c["slot"], c["mode"]
        pv = PSB[3].rearrange("p (c t) -> p c t", c=8)
        for k in range(8):
            S.op("pe", lambda e: e.transpose(out=pv[:, k, :], in_=xbA[:, k * 128:(k + 1) * 128], identity=ident),
                 reads=[r_xbA] + r_cst, writes=[PR[3]], inc=(k == 7))
        tt(hT, pv, gmix_t.unsqueeze(2).broadcast_to([128, 8, 128]), ALU.mult, [PR[3]] + r_cst, [r_hT])
        hc = [hT[:, k, :] for k in range(8)]
        rd = [r_hT] + r_WA
        c["hc"], c["rd"] = hc, rd
        if mode != "P":
            b = pbank()
            mm(PS[b], [(hc[k], Wkv[:, k, :]) for k in range(8)], rd, PR[b])
            act(kraw, PS[b][:, 0:256], AF.Copy, [PR[b]], [r_kraw])
            act(VV[:, t, :], PS[b][:, 256:512], AF.Copy, [PR[b]], [VVr[t]])

    def stage_A1b(c):
        t, s, mode = c["t"], c["slot"], c["mode"]
        hc, rd = c["hc"], c["rd"]
        b = pbank()
        mm(PS[b][:, 0:128], [(Wrank[:, k, :], hc[k]) for k in range(8)], rd, PR[b])
        act(augs[s][0:32, :], PS[b][0:32, 0:128], AF.Copy, [PR[b]], [aug_r[s]])
        b = pbank()
        mm(PS[b], [(hc[k], Wgla[:, k, 512:1024]) for k in range(8)], rd, PR[b])
        act(k_sb[s], PS[b], AF.Copy, [PR[b]], [k_r[s]])
        if mode != "P":
            headnorm_rope(kraw, r_kraw, 2, gk_t, tabC[s], tabS[s], tab_r[s], kbf, r_kbf)

    def stage_A2(c):
        t, s, mode = c["t"], c["slot"], c["mode"]
        hc, rd = c["hc"], c["rd"]
        if mode != "state":
            b = pbank()
            mm(PS[b], [(hc[k], Wgla[:, k, 0:512]) for k in range(8)], rd, PR[b])
            S.op("dve", lambda e: e.tensor_copy(out=q_sb[s], in_=PS[b]), reads=[PR[b]], writes=[q_r[s]])
        b = pbank()
        mm(PS[b], [(hc[k], Wgla[:, k, 1024:1536]) for k in range(8)], rd, PR[b])
        act(vbf[s][:, 0:512], PS[b], AF.Copy, [PR[b]], [v_r[s]])

    def stage_A2b(c):
        t, s, mode = c["t"], c["slot"], c["mode"]
        hc, rd = c["hc"], c["rd"]
        b = pbank()
        mm(PS[b], [(hc[k], Wgla[:, k, 1536:2048]) for k in range(8)], rd, PR[b])
        act(vbf[s][:, 512:1024], PS[b], AF.Copy, [PR[b]], [v_r[s]])
        if mode != "P":
            for g in range(2):
                S.op("pe", lambda e: e.transpose(out=PSB[3][:, g * 128:(g + 1) * 128], in_=kbf[:, g * 128:(g + 1) * 128], identity=ident),
                     reads=[r_kbf] + r_cst, writes=[PR[3]], inc=(g == 1))
            S.op("dve", lambda e: e.tensor_copy(out=KT[:, :, t * 128:(t + 1) * 128],
                                                in_=PSB[3][:, 0:256].rearrange("p (g t) -> p g t", g=2)),
                 reads=[PR[3]], writes=[KTr[t]])

    def stage_B1(c):
        s, dirn = c["slot"], c["dirn"]
        mm(PS[4], [(augs[s], waug_t[:, dirn, :])], [aug_r[s]] + r_cst, PR[4])
        act(SP, PS[4], AF.Exp, [PR[4]], [r_SP], scale=-1.0)
        act(SP, SP, AF.Ln, [r_SP], [r_SP], bias=1.0)
        S.op("dve", lambda e: e.tensor_copy(out=SPh, in_=SP), reads=[r_SP], writes=[r_SPh])
        tt(SPl, SP, SPh, ALU.subtract, [r_SP, r_SPh], [r_SPl])

    def stage_B2(c):
        s, dirn, mode = c["slot"], c["dirn"], c["mode"]
        own = mode != "state"
        ci, ri = (0, 1) if dirn == 0 else (2, 3)
        rds = [r_SPh, r_SPl] + r_cst
        if own:
            mm(PS[5], [(tri[:, ci, :], SPh), (tri[:, ci, :], SPl)], rds, PR[5])
        mm(PS[6], [(tri[:, ri, :], SPh), (tri[:, ri, :], SPl)], rds, PR[6])
        for h in range(4):
            mm(PS[4][:, h * 128:(h + 1) * 128], [(SPh[:, h * 128:(h + 1) * 128], nsix), (SPl[:, h * 128:(h + 1) * 128], nsix)], rds, PR[4])
        if own:
            act(Ecum, PS[5], AF.Exp, [PR[5]], [r_Ecum])
            act(Encum, PS[5], AF.Exp, [PR[5]], [r_Encum], scale=-1.0)
        act(Erem, PS[6], AF.Exp, [PR[6]], [r_Erem])
        act(Etot, PS[4].rearrange("p (h t) -> p h t", h=4)[:, :, 0], AF.Exp, [PR[4]], [r_Etot])
        if own:
            S.op("dve", lambda e: e.scalar_tensor_tensor(out=qt, in0=q_sb[s], scalar=float(128.0 ** -0.5), in1=Ecum,
                                                         op0=ALU.mult, op1=ALU.mult), reads=[q_r[s], r_Ecum], writes=[r_qt])
            tt(kt_, k_sb[s], Encum, ALU.mult, [k_r[s], r_Encum], [r_kt])
        tt(kd, k_sb[s], Erem, ALU.mult, [k_r[s], r_Erem], [r_kd])

    def stage_C(c):
        t, s, dirn, mode = c["t"], c["slot"], c["dirn"], c["mode"]
        own = mode != "state"
        if own:
            for h in range(4):
                S.op("pe", lambda e: e.transpose(out=PSB[7][:, h * 128:(h + 1) * 128], in_=qt[:, h * 128:(h + 1) * 128], identity=ident),
                     reads=[r_qt] + r_cst, writes=[PR[7]], inc=False)
            for h in range(4):
                S.op("pe", lambda e: e.transpose(out=PSB[7][:, 512 + h * 128:512 + (h + 1) * 128], in_=kt_[:, h * 128:(h + 1) * 128], identity=ident),
                     reads=[r_kt] + r_cst, writes=[PR[7]], inc=(h == 3))
            S.op("dve", lambda e: e.tensor_copy(out=qT, in_=PSB[7][:, 0:512].rearrange("p (h t) -> p h t", h=4)),
                 reads=[PR[7]], writes=[r_qT])
            S.op("dve", lambda e: e.tensor_copy(out=kT, in_=PSB[7][:, 512:1024].rearrange("p (h t) -> p h t", h=4)),
                 reads=[PR[7]], writes=[r_kT])
            for h in range(4):
                S.op("pe", lambda e: e.matmul(PS[4][:, h * 128:(h + 1) * 128], lhsT=kT[:, h, :], rhs=qT[:, h, :], start=True, stop=True),
                     reads=[r_qT, r_kT], writes=[PR[4]], inc=(h == 3))
            tt(Am, PS[4].rearrange("p (h t) -> p h t", h=4), mask[:, dirn, :].unsqueeze(1).broadcast_to([128, 4, 128]),
               ALU.mult, [PR[4]] + r_cst, [r_Am])
        for h in range(4):
            bank = 7 if h < 2 else 4
            S.op("pe", lambda e: e.matmul(PS[bank][:, (h % 2) * 256:(h % 2 + 1) * 256], lhsT=kd[:, h * 128:(h + 1) * 128],
                                          rhs=vbf[s][:, h * 256:(h + 1) * 256], start=True, stop=True),
                 reads=[r_kd, v_r[s]], writes=[PR[bank]], inc=True)
        if own:
            for h in range(4):
                bank = 5 + h // 2
                o_ap = PS[bank][:, (h % 2) * 256:(h % 2 + 1) * 256]
                S.op("pe", lambda e: e.matmul(o_ap, lhsT=Am[:, h, :], rhs=vbf[s][:, h * 256:(h + 1) * 256], start=True, stop=False),
                     reads=[r_Am, v_r[s]], writes=[PR[bank]], inc=False)
                S.op("pe", lambda e: e.matmul(o_ap, lhsT=qT[:, h, :], rhs=state_bf[:, h, :], start=False, stop=True),
                     reads=[r_qT, r_state_bf], writes=[PR[bank]], inc=True)

    def stage_Cs(c):
        for h in range(4):
            bank = 7 if h < 2 else 4
            S.op("dve", lambda e: e.scalar_tensor_tensor(out=state[:, h, :], in0=state[:, h, :], scalar=Etot[:, h:h + 1],
                                                         in1=PS[bank][:, (h % 2) * 256:(h % 2 + 1) * 256],
                                                         op0=ALU.mult, op1=ALU.add),
                 reads=[r_state, r_Etot, PR[bank]], writes=[r_state])
        S.op("pool", lambda e: e.tensor_copy(out=state_bf.rearrange("p h v -> p (h v)"), in_=state.rearrange("p h v -> p (h v)")),
             reads=[r_state], writes=[r_state_bf])

    def stage_D(c):
        t, s, mode = c["t"], c["slot"], c["mode"]
        if mode == "N":
            act(ON[:, t, 0:512], PS[5], AF.Copy, [PR[5]], [ONr[t]])
            act(ON[:, t, 512:1024], PS[6], AF.Copy, [PR[6]], [ONr[t]])
        if mode == "P":
            tt(osum[:, 0:512], PS[5], ON[:, t, 0:512], ALU.add, [PR[5], ONr[t]], [r_osum])
            tt(osum[:, 512:1024], PS[6], ON[:, t, 512:1024], ALU.add, [PR[6], ONr[t]], [r_osum])
            for h in range(4):
                act(junkC, osum[:, h * 256:(h + 1) * 256], AF.Square, [r_osum], [r_junkC, r_ssqC], accum_out=ssqC[:, h:h + 1])
            rstd_inplace(ssqC, r_ssqC, 256)
            for h in range(4):
                S.op("dve", lambda e: e.scalar_tensor_tensor(out=ya[:, h * 256:(h + 1) * 256], in0=osum[:, h * 256:(h + 1) * 256],
                                                             scalar=ssqC[:, h:h + 1], in1=gla_t, op0=ALU.mult, op1=ALU.mult),
                     reads=[r_osum, r_ssqC] + r_cst, writes=[r_ya])

    def stage_D2(c):
        t, mode = c["t"], c["mode"]
        if mode == "P":
            pv = PSB[7].rearrange("p (c t) -> p c t", c=8)
            for k in range(8):
                S.op("pe", lambda e: e.transpose(out=pv[:, k, :], in_=ya[:, k * 128:(k + 1) * 128], identity=ident),
                     reads=[r_ya] + r_cst, writes=[PR[7]], inc=(k == 7))
            S.op("dve", lambda e: e.tensor_copy(out=ON[:, t, :].rearrange("p (c t) -> p c t", c=8), in_=pv),
                 reads=[PR[7]], writes=[ONr[t]])

    def run_sequence(tiles, dirn, mode_of, tables):
        ctxs = [{"t": t, "slot": i % 2, "dirn": dirn, "mode": mode_of(t)} for i, t in enumerate(tiles)]
        n = len(ctxs)
        if n == 0:
            return
        load_tile(tiles[0], 0, tables)
        if n > 1:
            load_tile(tiles[1], 1, tables)
        stage_A0(ctxs[0])
        stage_A1(ctxs[0])
        stage_A1b(ctxs[0])
        stage_A2(ctxs[0])
        stage_A2b(ctxs[0])
        if n > 1:
            stage_A0(ctxs[1])
        for i in range(n):
            emit_conversion(1)
            if i + 2 < n:
                load_tile(tiles[i + 2], i % 2, tables)
            if i + 1 < n:
                stage_A1(ctxs[i + 1])
            if i >= 1:
                stage_Cs(ctxs[i - 1])
                stage_D(ctxs[i - 1])
            stage_B1(ctxs[i])
            if i + 1 < n:
                stage_A1b(ctxs[i + 1])
            if i + 2 < n:
                stage_A0(ctxs[i + 2])
            if i + 1 < n:
                stage_A2(ctxs[i + 1])
            stage_B2(ctxs[i])
            if i + 1 < n:
                stage_A2b(ctxs[i + 1])
            if i >= 1:
                stage_D2(ctxs[i - 1])
            stage_C(ctxs[i])
        stage_Cs(ctxs[n - 1])
        stage_D(ctxs[n - 1])
        stage_D2(ctxs[n - 1])

    order = list(range(NT - 1, -1, -1))
    n3a = NTO
    if isinstance(stage, tuple):
        order = order[stage[0]:stage[1]]
        n3a = stage[2]
    run_sequence(order, 1, lambda t: "state" if t >= NTO else "N", True)
    S.op("dve", lambda e: e.memset(state, 0.0), reads=[r_state], writes=[r_state])
    S.op("dve", lambda e: e.memset(state_bf, 0.0), reads=[r_state_bf], writes=[r_state_bf])
    run_sequence(list(range(n3a)), 0, lambda t: "P", False)

    emit_conversion(NBLK)
    S.barrier()
    stA.close()

    stB = ExitStack()
    NRING = 4
    ring = [sb([128, 8, 512], BF16, "ring%d" % i, stB) for i in range(NRING)]
    ring_r = [R("ring%d" % i) for i in range(NRING)]
    xg = sb([128, 4, 1024], F32, "xg", stB)
    xg_r = [R("xg%d" % i) for i in range(4)]
    tC = [sb([128, 128], F32, "tC%d" % i, stB) for i in range(4)]
    tS = [sb([128, 128], F32, "tS%d" % i, stB) for i in range(4)]
    tCS_r = [R("tCS%d" % i) for i in range(4)]
    hTg = sb([128, 8, 512], BF16, "hTg", stB)
    hTg_r = [R("hTg%d" % i) for i in range(4)]
    big = sb([128, 32, 512], BF16, "big", stB)
    big_r = [R("big%d" % i) for i in range(32)]
    QT = big[:, 0:8, :]
    mixT = sb([128, 8, 512], BF16, "mixT", stB)
    mixT_r = [R("mixT%d" % i) for i in range(4)]
    xtmp = sb([128, 1024], F32, "xtmp", stB)
    r_xtmp = R("xtmp")
    qbfs = [sb([128, 512], BF16, "qbf%d" % i, stB) for i in range(2)]
    qbf_r = [R("qbf%d" % i) for i in range(2)]
    NPT = 6
    PT = [sb([128, 512], BF16, "PT%d" % i, stB) for i in range(NPT)]
    PT_r = [R("PT%d" % i) for i in range(NPT)]

    rden = sb([128, 512], F32, "rden", stB)
    r_rden = R("rden")
    ybt = sb([128, 512], F32, "ybt", stB)
    r_ybt = R("ybt")
    yat = sb([128, 512], F32, "yat", stB)
    r_yat = R("yat")
    sgt = sb([128, 512], F32, "sgt", stB)
    r_sgt = R("sgt")
    _sg16 = sgt.bitcast(BF16)
    dhi, dlo = _sg16[:, 0:512], _sg16[:, 512:1024]
    r_dhi = r_dlo = r_sgt

    all_blocks = []
    for g in range(n_groups):
        all_blocks += [(nm, b, shp) for b, (nm, src, shp) in enumerate(group_blocks)]
    stream = {"next_load": 0, "next_use": 0, "released": 0}

    def ring_view(i, shape):
        return ring[i % NRING][:, 0:shape[1], 0:shape[2]]

    def issue_loads():
        while stream["next_load"] < len(all_blocks) and stream["next_load"] < stream["released"] + NRING:
            i = stream["next_load"]
            name, b, shape = all_blocks[i]
            S.dma("pool", ring_view(i, shape), blk_view(b), reads=[conv_r[b]], writes=[ring_r[i % NRING]])
            stream["next_load"] += 1

    def next_block(expect):
        i = stream["next_use"]
        name, src, shape = all_blocks[i]
        assert name.startswith(expect), (name, expect)
        assert i < stream["next_load"], "block not loaded yet (ring too small)"
        stream["next_use"] += 1
        return ring_view(i, shape), ring_r[i % NRING]

    def release(n=1):
        stream["released"] += n
        issue_loads()

    if isinstance(stage, tuple) or stage < 5:
        do_phase_b = False
    if do_phase_b:
        issue_loads()

    out_toks = []
    for g in range(n_groups if do_phase_b else 0):
        T0 = g * 4
        for i in range(4):
            t = T0 + i
            S.dma("sp", xg[:, i, :], xs[t * 128:(t + 1) * 128, :], writes=[xg_r[i]])
        if g == 0:
            for i in range(4):
                t = T0 + i
                S.dma("sp", tC[i], ropeC[t * 128:(t + 1) * 128, :], writes=[tCS_r[i]])
                S.dma("sp", tS[i], ropeS[t * 128:(t + 1) * 128, :], writes=[tCS_r[i]])
            for i in range(4):
                norm_transpose(xg[:, i, :], xg_r[i], gmix_t, hTg[:, :, i * 128:(i + 1) * 128], hTg_r[i], 7)
        def gate_chunk(wg, wg_r, n, cc):
            ch = n * 4 + cc
            bank = 2 + (ch % 2)
            mm(PS[bank], [(wg[:, c, cc * 128:(cc + 1) * 128], hTg[:, c, :]) for c in range(8)], hTg_r + [wg_r], PR[bank])
            if ch < 8:
                act(big[:, 8 + ch, :], PS[bank], AF.Silu, [PR[bank]], [big_r[8 + ch]])
            else:
                act(big[:, 8 + ch, :], PS[bank], AF.Sigmoid, [PR[bank]] + r_cst, [big_r[8 + ch]], bias=bgate_t[:, ch - 8:ch - 7])

        for n in range(2):
            wqn, wqn_r = next_block("wq%d" % n)
            wga = [next_block("wg%d" % (2 * n)), next_block("wg%d" % (2 * n + 1))]
            def q_mm_hnr(i):
                mm(PS[i % 2], [(hTg[:, c, i * 128:(i + 1) * 128], wqn[:, c, :]) for c in range(8)], [hTg_r[i], wqn_r], PR[i % 2])
                headnorm_rope(PS[i % 2], PR[i % 2], 4, gq_t, tC[i], tS[i], tCS_r[i], qbfs[i % 2], qbf_r[i % 2])
            q_mm_hnr(0)
            for i in range(4):
                if i + 1 < 4:
                    q_mm_hnr(i + 1)
                for cc in range(2):
                    gate_chunk(wga[i // 2][0], wga[i // 2][1], 2 * n + i // 2, (i % 2) * 2 + cc)
                pv = PSB[7][:, 0:512].rearrange("p (c t) -> p c t", c=4)
                for c in range(4):
                    S.op("pe", lambda e: e.transpose(out=pv[:, c, :], in_=qbfs[i % 2][:, c * 128:(c + 1) * 128], identity=ident),
                         reads=[qbf_r[i % 2]] + r_cst, writes=[PR[7]], inc=(c == 3))
                S.op("dve", lambda e: e.tensor_copy(out=QT[:, 4 * n:4 * n + 4, i * 128:(i + 1) * 128], in_=pv), reads=[PR[7]], writes=big_r[4 * n:4 * n + 4])
            release(3)
        for n in range(4, 6):
            wg, wg_r = next_block("wg%d" % n)
            for cc in range(4):
                gate_chunk(wg, wg_r, n, cc)
            release(1)
        for c8 in range(8):
            tt(big[:, 8 + c8, :], big[:, 8 + c8, :], big[:, 16 + c8, :], ALU.mult, [big_r[8 + c8], big_r[16 + c8]], [big_r[8 + c8]], eng="pool")
        pending = [None]
        for i in range(4):
            t = T0 + i
            for kvg in range(2):
                accb = 2 + 2 * kvg
                rhs_q = QT[:, kvg * 4:(kvg + 1) * 4, i * 128:(i + 1) * 128]
                qres = big_r[kvg * 4:(kvg + 1) * 4]

                SB = (0, 1, 6, 7)

                def smm(kt):
                    b = SB[kt % 4]
                    S.op("pe", lambda e: e.matmul(PS[b].rearrange("p (h t) -> p h t", h=4), lhsT=KT[:, kvg, kt * 128:(kt + 1) * 128],
                                                  rhs=rhs_q, start=True, stop=True),
                         reads=[KTr[kt]] + qres, writes=[PR[b]], inc=True)
                for k0 in range(3):
                    smm(k0)
                if pending[0] is not None:
                    pending[0]()
                    pending[0] = None
                for kt in range(NT):
                    if kt + 3 < NT:
                        smm(kt + 3)
                    p = PT[kt % NPT]
                    pr = PT_r[kt % NPT]
                    act(p, PS[SB[kt % 4]], AF.Exp, [PR[SB[kt % 4]]], [pr])
                    S.op("pe", lambda e: e.matmul(PS[accb], lhsT=VV[:, kt, kvg * 128:(kvg + 1) * 128], rhs=p, start=(kt == 0), stop=(kt == NT - 1)),
                         reads=[VVr[kt], pr], writes=[PR[accb]], inc=True)
                    if kt % 4 == 3:
                        for j in range(4):
                            kk = kt - 3 + j
                            S.op("pe", lambda e: e.matmul(PS[accb + 1][32 * j:32 * (j + 1), :], lhsT=ones[:, 0:32], rhs=PT[kk % NPT],
                                                          start=(kk < 4), stop=(kk >= NT - 4), tile_position=(0, 32 * j)),
                                 reads=[PT_r[kk % NPT]] + r_cst, writes=[PR[accb + 1]], inc=(j == 3))
                S.op("dve", lambda e: e.tensor_copy(out=dhi, in_=PS[accb + 1]), reads=[PR[accb + 1]], writes=[r_dhi])
                tt(dlo, PS[accb + 1], dhi, ALU.subtract, [PR[accb + 1], r_dhi], [r_dlo])
                def epilogue(accb=accb, i=i, kvg=kvg, t=t):
                    mm(PS[accb + 1], [(c32, dhi), (c32, dlo)], [r_dhi, r_dlo] + r_cst, PR[accb + 1])
                    S.op("dve", lambda e: e.reciprocal(out=rden, in_=PS[accb + 1]), reads=[PR[accb + 1]], writes=[r_rden])
                    tt(ybt, PS[accb], rden, ALU.mult, [PR[accb], r_rden], [r_ybt])
                    yb3 = ybt.rearrange("p (h t) -> p h t", h=4)
                    ya3v = yat.rearrange("p (h t) -> p h t", h=4)
                    gb = big[:, 24 + kvg * 4:24 + (kvg + 1) * 4, i * 128:(i + 1) * 128]
                    ga = big[:, 8 + kvg * 4:8 + (kvg + 1) * 4, i * 128:(i + 1) * 128]
                    tt(yb3, yb3, gb, ALU.mult, [r_ybt] + big_r[24 + kvg * 4:24 + (kvg + 1) * 4], [r_ybt])
                    ya3 = ON[:, t, :].rearrange("p (c t) -> p c t", c=8)[:, kvg * 4:(kvg + 1) * 4, :]
                    tt(ya3v, ya3, ga, ALU.mult, [ONr[t]] + big_r[8 + kvg * 4:8 + (kvg + 1) * 4], [r_yat], eng="pool")
                    tt(mixT[:, kvg * 4:(kvg + 1) * 4, i * 128:(i + 1) * 128], yb3, ya3v, ALU.add, [r_ybt, r_yat], [mixT_r[i]])
                pending[0] = epilogue
        if pending[0] is not None:
            pending[0]()
            pending[0] = None
        wo = [next_block("wo0"), next_block("wo1")]

        def wo_tile(i):
            for n in range(2):
                bank = 6 + n
                mm(PS[bank], [(mixT[:, c, i * 128:(i + 1) * 128], wo[n][0][:, c, :]) for c in range(8)], [mixT_r[i], wo[n][1]], PR[bank])
                tt(xg[:, i, n * 512:(n + 1) * 512], xg[:, i, n * 512:(n + 1) * 512], PS[bank], ALU.add, [xg_r[i], PR[bank]], [xg_r[i]])
        wo_tile(0)
        for i in range(4):
            if i + 1 < 4:
                wo_tile(i + 1)
            else:
                release(2)
            norm_transpose(xg[:, i, :], xg_r[i], gffn_t, hTg[:, :, i * 128:(i + 1) * 128], hTg_r[i], 0)
        for jb in range(6):
            j0 = jb * 4
            nj = min(4, NJ - j0)
            wgt, wgt_r = next_block("wgu_g%d" % jb)
            wup, wup_r = next_block("wgu_u%d" % jb)
            for jj in range(nj):
                j = j0 + jj
                bg, bu = (0, 1) if j % 2 == 0 else (2, 3)
                mm(PS[bg], [(wgt[:, c, jj * 128:(jj + 1) * 128], hTg[:, c, :]) for c in range(8)], hTg_r + [wgt_r], PR[bg])
                mm(PS[bu], [(wup[:, c, jj * 128:(jj + 1) * 128], hTg[:, c, :]) for c in range(8)], hTg_r + [wup_r], PR[bu])
                act(sgt, PS[bg], AF.Silu, [PR[bg]], [r_sgt])
                tt(big[:, j, :], sgt, PS[bu], ALU.mult, [r_sgt, PR[bu]], [big_r[j]])
            release(2)
        for n in range(2):
            for jb in range(3):
                j0 = jb * 8
                nj = min(8, NJ - j0)
                wd, wd_r = next_block("wd%d_%d" % (n, jb))
                for i in range(4):
                    bank = 4 + i
                    for jj in range(nj):
                        j = j0 + jj
                        S.op("pe", lambda e: e.matmul(PS[bank], lhsT=big[:, j, i * 128:(i + 1) * 128], rhs=wd[:, jj, :],
                                                      start=(j == 0), stop=(j == NJ - 1)),
                             reads=[big_r[j], wd_r], writes=[PR[bank]], inc=(jj == nj - 1))
                release(1)
                if g + 1 < n_groups and (n, jb) in ((0, 0), (0, 1), (0, 2), (1, 0)):
                    i2 = {(0, 0): 0, (0, 1): 1, (0, 2): 2, (1, 0): 3}[(n, jb)]
                    t2 = (g + 1) * 4 + i2
                    S.dma("sp", xtmp, xs[t2 * 128:(t2 + 1) * 128, :], writes=[r_xtmp])
                    S.dma("sp", tC[i2], ropeC[t2 * 128:(t2 + 1) * 128, :], writes=[tCS_r[i2]])
                    S.dma("sp", tS[i2], ropeS[t2 * 128:(t2 + 1) * 128, :], writes=[tCS_r[i2]])
                    norm_transpose(xtmp, r_xtmp, gmix_t, hTg[:, :, i2 * 128:(i2 + 1) * 128], hTg_r[i2], 0)
            for i in range(4):
                bank = 4 + i
                tt(xg[:, i, n * 512:(n + 1) * 512], xg[:, i, n * 512:(n + 1) * 512], PS[bank], ALU.add, [xg_r[i], PR[bank]], [xg_r[i]])
        for i in range(4):
            t = T0 + i
            act(junk, xg[:, i, :], AF.Square, [xg_r[i]], [r_junk, r_ssq], accum_out=ssq[:, 0:1])
            rstd_inplace(ssq[:, 0:1], r_ssq, D)
            S.op("dve", lambda e: e.scalar_tensor_tensor(out=xg[:, i, :], in0=xg[:, i, :], scalar=ssq[:, 0:1], in1=gfin_t, op0=ALU.mult, op1=ALU.mult),
                 reads=[xg_r[i], r_ssq] + r_cst, writes=[xg_r[i]])
            out_toks.append(S.dma("sp", y[t * 128:(t + 1) * 128, :], xg[:, i, :], reads=[xg_r[i]]))

    S._wait("sp", out_toks)
    S.barrier()
    stB.close()
    stack_main.close()
    return nc, S


def _rope_tables():
    rows = SEQ // GRID_W
    row = np.repeat(np.arange(rows), GRID_W).astype(np.float32)
    col = np.tile(np.arange(GRID_W), rows).astype(np.float32)
    freqs = (np.float32(10000.0) ** (-np.arange(0, 64, 2, dtype=np.float32) / np.float32(64))).astype(np.float32)
    ar = row[:, None] * freqs[None, :]
    ac = col[:, None] * freqs[None, :]
    cr, sr, cc, sc = np.cos(ar), np.sin(ar), np.cos(ac), np.sin(ac)
    Ct = np.concatenate([cr, cr, cc, cc], axis=1).astype(np.float32)
    St = np.concatenate([-sr, sr, -sc, sc], axis=1).astype(np.float32)
    return Ct, St


def _consts():
    m = np.arange(128)[:, None]
    i = np.arange(128)[None, :]
    s = np.float32(-1.0 / 16.0)
    tri = np.stack([(m <= i), (m > i), (m >= i), (m < i)], axis=1).astype(np.float32) * s
    mask = np.stack([(m <= i), (m > i)], axis=1).astype(np.float32)
    return np.ascontiguousarray(tri), np.ascontiguousarray(mask), np.eye(128, dtype=np.float32)


_PROGRAM = {}


def _make_in_maps(x, mix_norm, w_in, w_gk_fwd, b_gk_fwd, w_gk_bwd, b_gk_bwd, gla_norm,
                  q_norm, k_norm, b_gate, w_o, ffn_norm, w_gate_up, w_down, final_norm):
    f = lambda a: np.ascontiguousarray(np.asarray(a, dtype=np.float32))
    x = f(x)
    w_in0 = f(w_in)[0]
    Ct, St = _rope_tables()
    tri, mask, ident = _consts()
    col = lambda v: np.ascontiguousarray(f(v).reshape(-1, 128).T)
    bc = lambda v: np.ascontiguousarray(np.broadcast_to(f(v).reshape(1, -1), (128, f(v).size)))
    zeros16 = np.zeros((16, 512), np.float32)
    common = {
        "w_in": w_in0, "w_o": f(w_o)[0], "w_gu": f(w_gate_up)[0], "w_dn": f(w_down)[0],
        "gmix": col(mix_norm[0]), "gffn": col(ffn_norm[0]), "bgate": col(np.asarray(b_gate)[0].reshape(-1)),
        "gq": bc(q_norm[0]), "gk": bc(k_norm[0]), "gla": bc(gla_norm[0]), "gfin": bc(final_norm),
        "ident": ident, "tri": tri, "mask": mask,
    }
    wf, bf_, wb, bb = f(w_gk_fwd)[0], f(b_gk_fwd)[0], f(w_gk_bwd)[0], f(b_gk_bwd)[0]
    in_maps = []
    for c in range(8):
        b, half = c // 2, c % 2
        m = dict(common)
        if half == 0:
            m["xs"] = x[b]
            m["ropeC"], m["ropeS"] = Ct, St
            wP, bP, wN, bN = wf, bf_, wb, bb
            m["w_rank"] = np.ascontiguousarray(w_in0[:, C_RF:C_RF + 32])
        else:
            m["xs"] = np.ascontiguousarray(x[b, ::-1, :])
            m["ropeC"], m["ropeS"] = np.ascontiguousarray(Ct[::-1]), np.ascontiguousarray(St[::-1])
            wP, bP, wN, bN = wb, bb, wf, bf_
            m["w_rank"] = np.ascontiguousarray(np.concatenate([w_in0[:, C_RB:C_RB + 16], w_in0[:, C_RF:C_RF + 16]], axis=1))
        augP = np.concatenate([wP, zeros16, bP[None, :]], axis=0)
        augN = np.concatenate([zeros16, wN, bN[None, :]], axis=0)
        m["waug"] = np.ascontiguousarray(np.stack([augP, augN], axis=0))
        in_maps.append(m)
    return in_maps


def kernel(**inputs):
    if "nc" not in _PROGRAM:
        _PROGRAM["nc"], _ = build_program()
    nc = _PROGRAM["nc"]
    in_maps = _make_in_maps(**inputs)
    res = run_bass_kernel_spmd(nc, in_maps, core_ids=list(range(8)))
    out = np.empty((4, SEQ, D), np.float32)
    for c in range(8):
        b, half = c // 2, c % 2
        yc = np.asarray(res.results[c]["y"], dtype=np.float32)
        if half == 0:
            out[b, :OWN] = yc
        else:
            out[b, OWN:] = yc[::-1]
    return out
```
